# Optimizing a Trainium2 kernel written in Bass

```python
import jax
import jax.numpy as jnp
from jax import lax
import numpy as np

D_MODEL = 1024
BATCH = 4
SEQ = 4096
DEPTH = 2

NORM_EPS = 1e-6
ROPE_BASE = 10000.0

RW_HEADS = 4
RW_HEAD_DIM = 64
RW_DIM = RW_HEADS * RW_HEAD_DIM
RW_DECAY_RANK = 32
RW_A_RANK = 32
RW_V_RANK = 32
RW_GATE_RANK = 64
RW_GN_EPS = 64e-5
RW_IN = 3 * RW_DIM + RW_DECAY_RANK + RW_A_RANK + RW_GATE_RANK

RET_HEADS = 4
RET_HEAD_DIM = 64
RET_DIM = RET_HEADS * RET_HEAD_DIM
RET_CHUNK = 128
RET_IN = 4 * RET_DIM

MLA_HEADS = 8
MLA_NOPE = 64
MLA_ROPE = 32
MLA_V = 64
MLA_Q_RANK = 384
MLA_KV_RANK = 256
MLA_DIM = MLA_HEADS * MLA_V
MLA_IN = MLA_Q_RANK + MLA_KV_RANK + MLA_ROPE
ATTN_BLOCK = 128

IN_DIM = RW_IN + RET_IN + MLA_IN
MIX_DIM = RW_DIM + RET_DIM + MLA_DIM
D_FF = -(-(8 * D_MODEL) // (3 * 256)) * 256

kernel_name = 'hymba_rwkv7_retnet_mla_block'


def rms_norm(x, g):
    xf = x.astype(jnp.float32)
    y = xf * lax.rsqrt(jnp.mean(xf * xf, axis=-1, keepdims=True) + NORM_EPS)
    return (y * g.astype(jnp.float32)).astype(x.dtype)


def rope(x, positions):
    d = x.shape[-1]
    inv = ROPE_BASE ** (-jnp.arange(0, d, 2, dtype=jnp.float32) / d)
    ang = positions.astype(jnp.float32)[..., None] * inv
    ang = ang.reshape(ang.shape[:2] + (1,) * (x.ndim - 3) + ang.shape[-1:])
    cos, sin = jnp.cos(ang), jnp.sin(ang)
    xf = x.astype(jnp.float32)
    x1, x2 = xf[..., : d // 2], xf[..., d // 2:]
    return jnp.concatenate([x1 * cos - x2 * sin, x2 * cos + x1 * sin], axis=-1).astype(x.dtype)


def token_shift(p):
    return jnp.pad(p, ((0, 0), (1, 0), (0, 0)))[:, :-1]


def rwkv7_mixer(p, mu, w0, w2, a0, a2, g2, k_k, k_a, r_k, gn_w, gn_b, v_first, v_res):
    B, S, _ = p.shape
    p = p.astype(jnp.float32)
    ps = p + (token_shift(p) - p) * mu
    o1, o2, o3 = RW_DIM, 2 * RW_DIM, 3 * RW_DIM
    o4, o5 = o3 + RW_DECAY_RANK, o3 + RW_DECAY_RANK + RW_A_RANK
    r, k, v = ps[..., :o1], ps[..., o1:o2], ps[..., o2:o3]
    wd, ad, gd = ps[..., o3:o4], ps[..., o4:o5], ps[..., o5:]
    w = -jax.nn.softplus(-(w0 + jnp.tanh(wd) @ w2)) - 0.5
    decay = jnp.exp(-jnp.exp(w))
    a = jax.nn.sigmoid(a0 + ad @ a2)
    g = jax.nn.sigmoid(gd) @ g2
    if v_res is None:
        v_first = v
    else:
        v0, v1, v2 = v_res
        v = v + (v_first - v) * jax.nn.sigmoid(v0 + (v @ v1) @ v2)
    heads = lambda t: t.reshape(B, S, RW_HEADS, RW_HEAD_DIM)
    kk = heads(k * k_k)
    kk = kk * lax.rsqrt(jnp.maximum(jnp.sum(kk * kk, axis=-1, keepdims=True), 1e-24))
    k = k * (1.0 + (a - 1.0) * k_a)
    r_h, k_h, v_h, w_h, a_h = heads(r), heads(k), heads(v), heads(decay), heads(a)

    def step(state, inp):
        rt, kt, vt, wt, kkt, at = inp
        sa = jnp.einsum('bhvk,bhk->bhv', state, -kkt)
        state = (state * wt[:, :, None, :] + sa[..., None] * (kkt * at)[:, :, None, :]
                 + vt[..., None] * kt[:, :, None, :])
        return state, jnp.einsum('bhvk,bhk->bhv', state, rt)

    xs = tuple(jnp.moveaxis(t, 1, 0) for t in (r_h, k_h, v_h, w_h, kk, a_h))
    s0 = jnp.zeros((B, RW_HEADS, RW_HEAD_DIM, RW_HEAD_DIM), jnp.float32)
    _, y = lax.scan(step, s0, xs)
    y = jnp.moveaxis(y, 0, 1)
    mean = jnp.mean(y, axis=-1, keepdims=True)
    var = jnp.mean(jnp.square(y - mean), axis=-1, keepdims=True)
    y = ((y - mean) * lax.rsqrt(var + RW_GN_EPS)).reshape(B, S, RW_DIM) * gn_w + gn_b
    bonus = jnp.sum(r_h * k_h * r_k, axis=-1, keepdims=True) * v_h
    y = (y + bonus.reshape(B, S, RW_DIM)) * g
    return y, v_first


def retention_mixer(p, positions):
    B, S, _ = p.shape
    p = p.astype(jnp.float32)
    q, k, v, g = jnp.split(p, 4, axis=-1)
    heads = lambda t: t.reshape(B, S, RET_HEADS, RET_HEAD_DIM)
    q = rope(heads(q), positions)
    k = rope(heads(k), positions) * (RET_HEAD_DIM ** -0.5)
    v = heads(v)
    log_gamma = jnp.log(1.0 - 2.0 ** (-5.0 - jnp.arange(RET_HEADS, dtype=jnp.float32)))
    n_chunks = S // RET_CHUNK
    chunk = lambda t: t.reshape(B, n_chunks, RET_CHUNK, RET_HEADS, RET_HEAD_DIM).transpose(0, 3, 1, 2, 4)
    qc, kc, vc = chunk(q), chunk(k), chunk(v)
    idx = jnp.arange(RET_CHUNK, dtype=jnp.float32)
    diff = idx[:, None] - idx[None, :]
    dmask = jnp.where(diff >= 0, jnp.exp(log_gamma[:, None, None] * jnp.maximum(diff, 0.0)), 0.0)
    inner = jnp.einsum('bhnqd,bhnkd->bhnqk', qc, kc) * dmask[:, None]
    inner = jnp.einsum('bhnqk,bhnkd->bhnqd', inner, vc)
    q_dec = jnp.exp(log_gamma[:, None] * (idx + 1.0))
    k_dec = jnp.exp(log_gamma[:, None] * (RET_CHUNK - 1.0 - idx))
    chunk_dec = jnp.exp(log_gamma * RET_CHUNK)
    kv_chunk = jnp.einsum('bhnkd,bhnke->nbhde', kc * k_dec[:, None, :, None], vc)

    def step(R, kv_i):
        return R * chunk_dec[None, :, None, None] + kv_i, R

    R0 = jnp.zeros((B, RET_HEADS, RET_HEAD_DIM, RET_HEAD_DIM), jnp.float32)
    _, R_prev = lax.scan(step, R0, kv_chunk)
    cross = jnp.einsum('bhnqd,nbhde->bhnqe', qc * q_dec[:, None, :, None], R_prev)
    y = (inner + cross).transpose(0, 2, 3, 1, 4).reshape(B, S, RET_HEADS, RET_HEAD_DIM)
    y = y * lax.rsqrt(jnp.mean(y * y, axis=-1, keepdims=True) + NORM_EPS)
    return jax.nn.silu(g) * y.reshape(B, S, RET_DIM)


def mla_mixer(p, positions, q_norm, kv_norm, w_q_up, w_kv_up):
    B, S, _ = p.shape
    q_lat = p[..., :MLA_Q_RANK]
    kv_lat = p[..., MLA_Q_RANK:MLA_Q_RANK + MLA_KV_RANK]
    k_rope = rope(p[..., MLA_Q_RANK + MLA_KV_RANK:], positions)
    q = (rms_norm(q_lat, q_norm) @ w_q_up).reshape(B, S, MLA_HEADS, MLA_NOPE + MLA_ROPE)
    q_nope, q_rope = q[..., :MLA_NOPE], rope(q[..., MLA_NOPE:], positions)
    kv = (rms_norm(kv_lat, kv_norm) @ w_kv_up).reshape(B, S, MLA_HEADS, MLA_NOPE + MLA_V)
    k_nope, v = kv[..., :MLA_NOPE], kv[..., MLA_NOPE:]
    scale = (MLA_NOPE + MLA_ROPE) ** -0.5
    n_blocks = S // ATTN_BLOCK
    blk = lambda t: t.reshape(B, n_blocks, ATTN_BLOCK, MLA_HEADS, t.shape[-1]).transpose(1, 0, 3, 2, 4)
    qn_b, qr_b = blk(q_nope), blk(q_rope)
    key_idx = jnp.arange(S)

    def attend(args):
        qn, qr, i = args
        s = (jnp.einsum('bhqd,bkhd->bhqk', qn, k_nope)
             + jnp.einsum('bhqr,bkr->bhqk', qr, k_rope)).astype(jnp.float32) * scale
        q_idx = i * ATTN_BLOCK + jnp.arange(ATTN_BLOCK)
        s = jnp.where(key_idx[None, :] <= q_idx[:, None], s, -jnp.inf)
        pr = jax.nn.softmax(s, axis=-1).astype(v.dtype)
        return jnp.einsum('bhqk,bkhd->bqhd', pr, v)

    o = lax.map(attend, (qn_b, qr_b, jnp.arange(n_blocks)))
    return o.transpose(1, 0, 2, 3, 4).reshape(B, S, MLA_DIM)


def setup_inputs(seed: int = 0) -> dict:
    key = jax.random.key(seed)
    keys = jax.random.split(key, 32)
    nrm = lambda i, shape, scale: scale * jax.random.normal(keys[i], shape, jnp.float32)
    L = DEPTH
    ch = jnp.linspace(0.0, 1.0, RW_DIM, dtype=jnp.float32)
    return {
        'x': nrm(0, (BATCH, SEQ, D_MODEL), 1.0),
        'positions': jnp.broadcast_to(jnp.arange(SEQ, dtype=jnp.int32), (BATCH, SEQ)),
        'attn_norm': 1.0 + nrm(1, (L, D_MODEL), 0.05),
        'w_in': nrm(2, (L, D_MODEL, IN_DIM), D_MODEL ** -0.5),
        'w_out': nrm(3, (L, MIX_DIM, D_MODEL), MIX_DIM ** -0.5),
        'rw_mu': jax.random.uniform(keys[4], (L, RW_IN), jnp.float32),
        'rw_w0': -6.0 + 5.0 * ch ** 0.85 + nrm(5, (L, RW_DIM), 0.1),
        'rw_w2': nrm(6, (L, RW_DECAY_RANK, RW_DIM), 0.5 * RW_DECAY_RANK ** -0.5),
        'rw_a0': nrm(7, (L, RW_DIM), 0.1),
        'rw_a2': nrm(8, (L, RW_A_RANK, RW_DIM), RW_A_RANK ** -0.5),
        'rw_g2': nrm(9, (L, RW_GATE_RANK, RW_DIM), RW_GATE_RANK ** -0.5),
        'rw_k_k': 0.85 + nrm(10, (L, RW_DIM), 0.05),
        'rw_k_a': 1.0 + nrm(11, (L, RW_DIM), 0.05),
        'rw_r_k': nrm(12, (L, RW_HEADS, RW_HEAD_DIM), 0.1),
        'rw_gn_w': 1.0 + nrm(13, (L, RW_DIM), 0.05),
        'rw_gn_b': nrm(14, (L, RW_DIM), 0.01),
        'rw_v0': 1.0 + nrm(15, (L - 1, RW_DIM), 0.1),
        'rw_v1': nrm(16, (L - 1, RW_DIM, RW_V_RANK), RW_DIM ** -0.5),
        'rw_v2': nrm(17, (L - 1, RW_V_RANK, RW_DIM), 0.5 * RW_V_RANK ** -0.5),
        'mla_q_norm': 1.0 + nrm(18, (L, MLA_Q_RANK), 0.05),
        'mla_kv_norm': 1.0 + nrm(19, (L, MLA_KV_RANK), 0.05),
        'mla_w_q_up': nrm(20, (L, MLA_Q_RANK, MLA_HEADS * (MLA_NOPE + MLA_ROPE)), MLA_Q_RANK ** -0.5),
        'mla_w_kv_up': nrm(21, (L, MLA_KV_RANK, MLA_HEADS * (MLA_NOPE + MLA_V)), MLA_KV_RANK ** -0.5),
        'ffn_norm': 1.0 + nrm(22, (L, D_MODEL), 0.05),
        'w_gate_up': nrm(23, (L, D_MODEL, 2 * D_FF), D_MODEL ** -0.5),
        'w_down': nrm(24, (L, D_FF, D_MODEL), D_FF ** -0.5),
        'final_norm': 1.0 + nrm(25, (D_MODEL,), 0.05),
    }


def reference(x, positions, attn_norm, w_in, w_out, rw_mu, rw_w0, rw_w2, rw_a0, rw_a2, rw_g2,
              rw_k_k, rw_k_a, rw_r_k, rw_gn_w, rw_gn_b, rw_v0, rw_v1, rw_v2, mla_q_norm, mla_kv_norm,
              mla_w_q_up, mla_w_kv_up, ffn_norm, w_gate_up, w_down, final_norm):
    h = x
    v_first = None
    for l in range(DEPTH):
        hn = rms_norm(h, attn_norm[l])
        proj = hn @ w_in[l]
        p_a = proj[..., :RW_IN]
        p_b = proj[..., RW_IN:RW_IN + RET_IN]
        p_c = proj[..., RW_IN + RET_IN:]
        v_res = None if l == 0 else (rw_v0[l - 1], rw_v1[l - 1], rw_v2[l - 1])
        y_a, v_first = rwkv7_mixer(p_a, rw_mu[l], rw_w0[l], rw_w2[l], rw_a0[l], rw_a2[l], rw_g2[l],
                                   rw_k_k[l], rw_k_a[l], rw_r_k[l], rw_gn_w[l], rw_gn_b[l], v_first, v_res)
        y_b = retention_mixer(p_b, positions)
        y_c = mla_mixer(p_c, positions, mla_q_norm[l], mla_kv_norm[l], mla_w_q_up[l], mla_w_kv_up[l])
        mix = jnp.concatenate([y_a.astype(h.dtype), y_b.astype(h.dtype), y_c.astype(h.dtype)], axis=-1)
        h = h + mix @ w_out[l]
        hn = rms_norm(h, ffn_norm[l])
        gu = hn @ w_gate_up[l]
        h = h + (jax.nn.silu(gu[..., :D_FF]) * gu[..., D_FF:]) @ w_down[l]
    return rms_norm(h, final_norm)
```

```python
import math
import numpy as np
from contextlib import ExitStack
import concourse.bass as bass
import concourse.mybir as mybir
from concourse.bass_utils import run_bass_kernel_spmd

F32 = mybir.dt.float32
BF16 = mybir.dt.bfloat16
I32 = mybir.dt.int32
AF = mybir.ActivationFunctionType
ALU = mybir.AluOpType
AX = mybir.AxisListType

ENGS = ("pe", "act", "dve", "pool", "sp")
NDSEM = 6

S_LEN = 4096
D = 1024
L = 2
NFM = 20 * 128 + 64
NW = NFM + 512
DFF = 2816
EPS = 1e-6


class Buf:
    __slots__ = ("w", "r")

    def __init__(self):
        self.w = None
        self.r = {}


class Sched:
    def __init__(self, nc, es):
        self.nc = nc
        self.ops = {e: [] for e in ENGS}
        self.cnt = {e: 0 for e in ENGS}
        self.sems = {}
        for e in ENGS:
            self.sems[e] = es.enter_context(nc.semaphore("c_" + e))
        self.dcnt = {}
        for e in ("sp", "pool", "act"):
            for i in range(NDSEM):
                k = "d_%s%d" % (e, i)
                self.sems[k] = es.enter_context(nc.semaphore(k))
                self.dcnt[k] = 0
        self.dnext = {"sp": 0, "pool": 0, "act": 0}
        self.waited = {e: {} for e in ENGS}

    def _waits(self, eng, deps):
        for (k, v) in sorted(deps):
            if eng == "pe" and k == "pe":
                continue
            if self.waited[eng].get(k, 0) < v:
                self.waited[eng][k] = v
                sem = self.sems[k]
                self.ops[eng].append(lambda e, sem=sem, v=v: e.wait_ge(sem, v))

    @staticmethod
    def _deps(reads, writes):
        deps = set()
        for b in reads:
            if b.w is not None:
                deps.add(b.w)
        for b in writes:
            if b.w is not None:
                deps.add(b.w)
            deps.update(b.r.values())
        return deps

    def op(self, eng, fn, reads=(), writes=()):
        self._waits(eng, self._deps(reads, writes))
        self.cnt[eng] += 1
        dep = (eng, self.cnt[eng])
        sem = self.sems[eng]
        self.ops[eng].append(lambda e, fn=fn, sem=sem: fn(e).then_inc(sem, 1))
        for b in reads:
            b.r[eng] = dep
        for b in writes:
            b.w = dep
            b.r = {}

    def dma(self, q, out, in_, reads=(), writes=(), **kw):
        i = self.dnext[q]
        self.dnext[q] = (i + 1) % NDSEM
        k = "d_%s%d" % (q, i)
        deps = self._deps(reads, writes)
        if self.dcnt[k] > 0:
            deps.add((k, 16 * self.dcnt[k]))
        self._waits(q, deps)
        self.dcnt[k] += 1
        dep = (k, 16 * self.dcnt[k])
        sem = self.sems[k]
        self.ops[q].append(
            lambda e, out=out, in_=in_, sem=sem, kw=kw: e.dma_start(out=out, in_=in_, **kw).then_inc(sem, 16))
        for b in reads:
            b.r["dma" + k] = dep
        for b in writes:
            b.w = dep
            b.r = {}

    def barrier(self):
        deps = set()
        for e in ENGS:
            if self.cnt[e] > 0:
                deps.add((e, self.cnt[e]))
        for k, c in self.dcnt.items():
            if c > 0:
                deps.add((k, 16 * c))
        for e in ENGS:
            self._waits(e, {d for d in deps if d[0] != e})

    def finish(self, out_bufs):
        deps = set()
        for b in out_bufs:
            if b.w is not None:
                deps.add(b.w)
        self._waits("sp", deps)
        ops = self.ops
        with self.nc.Block() as block:
            @block.tensor
            def _(e):
                for f in ops["pe"]:
                    f(e)

            @block.scalar
            def _(e):
                for f in ops["act"]:
                    f(e)

            @block.vector
            def _(e):
                for f in ops["dve"]:
                    f(e)

            @block.gpsimd
            def _(e):
                for f in ops["pool"]:
                    f(e)

            @block.sync
            def _(e):
                for f in ops["sp"]:
                    f(e)


def build(stages=None, dbg=()):
    nc = bass.Bass("TRN2", target_bir_lowering=False)

    def din(name, shape, dt=F32):
        return nc.dram_tensor(name, list(shape), dt, kind="ExternalInput").ap()

    def dscr(name, shape, dt=F32):
        kind = "ExternalOutput" if name in dbg else "Internal"
        return nc.dram_tensor(name, list(shape), dt, kind=kind).ap()

    x = din("x", [S_LEN, D])
    pos = din("pos", [S_LEN], I32)
    cst = din("cst", [128, 4])
    attn_norm = din("attn_norm", [L, D])
    w_in_p = din("w_in_p", [L, D, NW])
    w_out = din("w_out", [L, D, D])
    rw_mu = din("rw_mu", [L, 896])
    rw_w0 = din("rw_w0", [L, 256])
    rw_w2 = din("rw_w2", [L, 32, 256])
    rw_a0 = din("rw_a0", [L, 256])
    rw_a2 = din("rw_a2", [L, 32, 256])
    rw_g2 = din("rw_g2", [L, 64, 256])
    rw_k_k = din("rw_k_k", [L, 256])
    rw_k_a = din("rw_k_a", [L, 256])
    rw_r_k = din("rw_r_k", [L, 256])
    rw_gn_w = din("rw_gn_w", [L, 256])
    rw_gn_b = din("rw_gn_b", [L, 256])
    rw_v0 = din("rw_v0", [1, 256])
    rw_v1 = din("rw_v1", [1, 256, 32])
    rw_v2 = din("rw_v2", [1, 32, 256])
    q_norm = din("q_norm", [L, 384])
    kv_norm = din("kv_norm", [L, 256])
    wq_p = din("wq_p", [L, 384, 8 * 192])
    wkvk = din("wkvk", [L, 256, 512])
    wkvv = din("wkvv", [L, 256, 512])
    ffn_norm = din("ffn_norm", [L, D])
    w_gu = din("w_gu", [L, D, 2 * DFF])
    w_down = din("w_down", [L, DFF, D])
    final_norm = din("final_norm", [D])
    out = nc.dram_tensor("out", [S_LEN, D], F32, kind="ExternalOutput").ap()

    projT = dscr("projT", [NFM, S_LEN])
    projTM = dscr("projTM", [S_LEN, 512])
    ropeT = dscr("ropeT", [4, 128, S_LEN])
    qT = dscr("qT", [8, 96, S_LEN], BF16)
    kT = dscr("kT", [8, 96, S_LEN], BF16)
    vtok = dscr("vtok", [S_LEN, 512], BF16)
    mix = dscr("mix", [S_LEN, D])
    hbuf = dscr("hbuf", [S_LEN, D])
    wgub = dscr("wgub", [D, 2 * DFF], BF16)
    vfirst = dscr("vfirst", [256, S_LEN])

    b_projT = [[Buf() for _ in range(8)] for _ in range(22)]
    b_projTM = [Buf() for _ in range(32)]
    b_rope = [Buf() for _ in range(8)]
    b_qT = [Buf() for _ in range(8)]
    b_kT = [Buf() for _ in range(8)]
    b_vtok = Buf()
    b_mix = [Buf() for _ in range(32)]
    b_hbuf = [Buf() for _ in range(32)]
    b_wgub = Buf()
    b_vfirst = [Buf() for _ in range(8)]
    b_out = [Buf() for _ in range(32)]
    b_none = [Buf() for _ in range(32)]

    es = ExitStack()
    with es:
        S = Sched(nc, es)
        uid = [0]

        def sb(st, shape, dt, name=None):
            uid[0] += 1
            return st.enter_context(nc.sbuf_tensor("%s_%d" % (name or "t", uid[0]), list(shape), dt))

        PS = [es.enter_context(nc.psum_tensor("ps%d" % i, [128, 512], F32)) for i in range(6)]
        b_PS = [Buf() for _ in range(6)]
        PB = [es.enter_context(nc.psum_tensor("pb%d" % i, [128, 8, 128], BF16)) for i in range(2)]
        b_PB = [Buf() for _ in range(2)]
        rr = {"ps": 0, "pb": 0, "ev": 0}

        def next_ps():
            i = rr["ps"]
            rr["ps"] = (i + 1) % 6
            return PS[i], b_PS[i]

        def next_pb():
            i = rr["pb"]
            rr["pb"] = (i + 1) % 2
            return PB[i], b_PB[i]

        def ev_eng():
            rr["ev"] ^= 1
            return "act" if rr["ev"] else "dve"

        def mm(o, lt, rh, st_, sp_, R, W):
            S.op("pe", lambda e: e.matmul(o, lt, rh, start=st_, stop=sp_), reads=R, writes=W)

        def tr(o, i_, idt, R, W):
            S.op("pe", lambda e: e.transpose(o, i_, idt), reads=R, writes=W)

        def cp(eng, o, i_, R, W):
            if eng == "act":
                S.op("act", lambda e: e.copy(o, i_), reads=R, writes=W)
            else:
                S.op(eng, lambda e: e.tensor_copy(o, i_), reads=R, writes=W)

        def scl(eng, o, i_, sc, R, W):
            if eng == "act":
                S.op("act", lambda e: e.activation(o, i_, AF.Copy, scale=sc), reads=R, writes=W)
            else:
                S.op(eng, lambda e: e.tensor_scalar(o, i_, sc, None, ALU.mult), reads=R, writes=W)

        def tt(eng, o, a, b, op, R, W):
            S.op(eng, lambda e: e.tensor_tensor(o, a, b, op), reads=R, writes=W)

        def stt(o, a, sc, b, op0, op1, R, W, eng="dve"):
            S.op(eng, lambda e: e.scalar_tensor_tensor(o, a, sc, b, op0, op1), reads=R, writes=W)

        def act(o, i_, func, R, W, **kw):
            S.op("act", lambda e: e.activation(o, i_, func, **kw), reads=R, writes=W)

        def rstd_of(o, i_, scale, R, W, eps=EPS):
            act(o, i_, AF.Ln, R, W, scale=scale, bias=eps)
            act(o, o, AF.Exp, W, W, scale=-0.5)

        ident_f = sb(es, [128, 128], F32, "identf"); b_c = Buf()
        ident_b = sb(es, [128, 128], BF16, "identb")
        m_su = sb(es, [128, 128], F32, "msu")
        m_iu = sb(es, [128, 128], F32, "miu")
        m_sl = sb(es, [128, 128], F32, "msl")
        ones_f = sb(es, [128, 128], F32, "onesf")
        ones_bd = sb(es, [128, 128], F32, "onesbd")
        istack = sb(es, [128, 64], F32, "istack")
        cmask = sb(es, [128, 128], F32, "cmask")
        cstt = sb(es, [128, 4], F32, "cstt")
        S.dma("sp", cstt[:], cst[:, :], writes=[b_c])
        S.op("pool", lambda e: e.memset(ones_f[:], 1.0), writes=[b_c])
        S.op("pool", lambda e: e.memset(ones_bd[:], 0.0), writes=[b_c])
        S.op("pool", lambda e: e.memset(ones_bd[0:64, 0:64], 1.0), writes=[b_c])
        S.op("pool", lambda e: e.memset(ones_bd[64:128, 64:128], 1.0), writes=[b_c])
        S.op("pool", lambda e: e.memset(ident_f[:], 0.0), writes=[b_c])
        S.op("pool", lambda e: e.affine_select(ident_f[:], ident_f[:], pattern=[[-1, 128]], compare_op=ALU.not_equal,
                                               fill=1.0, base=0, channel_multiplier=1), reads=[b_c], writes=[b_c])
        cp("dve", ident_b[:], ident_f[:], [b_c], [b_c])
        S.op("pool", lambda e: e.affine_select(m_su[:], ones_f[:], pattern=[[1, 128]], compare_op=ALU.is_gt,
                                               fill=0.0, base=0, channel_multiplier=-1), reads=[b_c], writes=[b_c])
        S.op("pool", lambda e: e.affine_select(m_iu[:], ones_f[:], pattern=[[1, 128]], compare_op=ALU.is_ge,
                                               fill=0.0, base=0, channel_multiplier=-1), reads=[b_c], writes=[b_c])
        S.op("pool", lambda e: e.affine_select(m_sl[:], ones_f[:], pattern=[[-1, 128]], compare_op=ALU.is_gt,
                                               fill=0.0, base=0, channel_multiplier=1), reads=[b_c], writes=[b_c])
        S.op("pool", lambda e: e.memset(cmask[:], 0.0), writes=[b_c])
        S.op("pool", lambda e: e.affine_select(cmask[:], cmask[:], pattern=[[-1, 128]], compare_op=ALU.is_ge,
                                               fill=-1e30, base=0, channel_multiplier=1), reads=[b_c], writes=[b_c])
        cp("dve", istack[0:64, :], ident_f[0:64, 0:64], [b_c], [b_c])
        cp("dve", istack[64:128, :], ident_f[64:128, 64:128], [b_c], [b_c])

        def stage_rope():
            with ExitStack() as st:
                posi = sb(st, [128, 512], I32); b_posi = Buf()
                posf = sb(st, [128, 512], F32); b_posf = Buf()
                tt_ = [sb(st, [128, 512], F32) for _ in range(2)]; b_tt = [Buf(), Buf()]
                ki = sb(st, [128, 512], I32); b_ki = Buf()
                kf = sb(st, [128, 512], F32); b_kf = Buf()
                ot = [sb(st, [128, 512], F32) for _ in range(2)]; b_ot = [Buf(), Buf()]
                n = 0
                for tb in range(8):
                    S.dma("sp", posi[:], pos[tb * 512:(tb + 1) * 512].partition_broadcast(128), writes=[b_posi])
                    cp("dve", posf[:], posi[:], [b_posi], [b_posf])
                    for ti in range(4):
                        fcol = 0 if ti < 2 else 2
                        is_sin = ti % 2 == 1
                        t_, bt = tt_[n % 2], b_tt[n % 2]
                        o_, bo = ot[n % 2], b_ot[n % 2]
                        n += 1
                        S.op("dve", lambda e, t_=t_, fcol=fcol: e.tensor_scalar(t_[:], posf[:], cstt[:, fcol:fcol + 1], None, ALU.mult),
                             reads=[b_posf, b_c], writes=[bt])
                        off = 0.5 if is_sin else 0.75
                        S.op("dve", lambda e, t_=t_, off=off: e.tensor_scalar(t_[:], t_[:], 1.0 / (2 * math.pi), off, ALU.mult, ALU.add),
                             reads=[bt], writes=[bt])
                        cp("dve", ki[:], t_[:], [bt], [b_ki])
                        cp("dve", kf[:], ki[:], [b_ki], [b_kf])
                        tt("dve", t_[:], t_[:], kf[:], ALU.subtract, [bt, b_kf], [bt])
                        stt(t_[:], t_[:], 0.0, t_[:], ALU.is_lt, ALU.add, [bt], [bt])
                        S.op("dve", lambda e, t_=t_: e.tensor_scalar(t_[:], t_[:], 2 * math.pi, -math.pi, ALU.mult, ALU.add),
                             reads=[bt], writes=[bt])
                        act(o_[:], t_[:], AF.Sin, [bt], [bo])
                        if is_sin:
                            scl("pool", o_[:], o_[:], cstt[:, fcol + 1:fcol + 2], [bo, b_c], [bo])
                        S.dma("pool", ropeT[ti, :, tb * 512:(tb + 1) * 512], o_[:], reads=[bo], writes=[b_rope[tb]])

        def stage_p1(l, fuse_wgu=False):
            hsrc = x if l == 0 else hbuf
            b_hsrc = b_none if l == 0 else b_hbuf
            with ExitStack() as st:
                winb = sb(st, [128, 8, NW], BF16, "winb"); b_winb = Buf()
                gt = sb(st, [128, 8], F32); b_gt = Buf()
                S.dma("sp", gt[:], attn_norm[l].rearrange("(c p) -> p c", p=128), writes=[b_gt], allow_slow_non_contiguous=True)
                wst = [sb(st, [128, 8, 512], F32) for _ in range(2)]; b_wst = [Buf(), Buf()]
                for j in range(7):
                    c0 = j * 512
                    cw = min(512, NW - c0)
                    k = j % 2
                    S.dma("sp", wst[k][:, :, 0:cw], w_in_p[l, :, c0:c0 + cw].rearrange("(c p) n -> p c n", p=128), writes=[b_wst[k]])
                    for c in range(8):
                        scl(("act", "pool", "dve")[c % 3], winb[:, c, c0:c0 + cw], wst[k][:, c, 0:cw], gt[:, c:c + 1],
                            [b_wst[k], b_gt], [b_winb])
                ht = [sb(st, [128, D], F32) for _ in range(2)]; b_ht = [Buf(), Buf()]
                junk = sb(st, [128, D], BF16); b_junk = Buf()
                ss = [sb(st, [128, 1], F32) for _ in range(2)]; b_ss = [Buf(), Buf()]
                hnb = [sb(st, [128, D], BF16) for _ in range(2)]; b_hnb = [Buf(), Buf()]
                hnT = [sb(st, [128, 8, 512], BF16) for _ in range(2)]; b_hnT = [Buf(), Buf()]
                stg = [sb(st, [128, 512], F32) for _ in range(4)]; b_stg = [Buf() for _ in range(4)]
                ns = 0
                wgen = wgu_gen(l, st) if fuse_wgu else iter(())
                for tb in range(8):
                    for _ in range(2 if tb < 3 else 1):
                        next(wgen, None)
                    hT, bhT = hnT[tb % 2], b_hnT[tb % 2]
                    for i in range(4):
                        ti = tb * 4 + i
                        k = ti % 2
                        S.dma("sp", ht[k][:], hsrc[ti * 128:(ti + 1) * 128, :], reads=[b_hsrc[ti]], writes=[b_ht[k]])
                        act(junk[:], ht[k][:], AF.Square, [b_ht[k]], [b_junk, b_ss[k]], accum_out=ss[k][:, 0:1])
                        rstd_of(ss[k][:], ss[k][:], 1.0 / D, [b_ss[k]], [b_ss[k]])
                        scl("dve", hnb[k][:], ht[k][:], ss[k][:, 0:1], [b_ht[k], b_ss[k]], [b_hnb[k]])
                        pb, bpb = next_pb()
                        for c in range(8):
                            tr(pb[:, c, :], hnb[k][:, c * 128:(c + 1) * 128], ident_b[:], [b_hnb[k], b_c], [bpb])
                        cp(ev_eng(), hT[:, :, i * 128:(i + 1) * 128], pb[:], [bpb], [bhT])
                    for m in range(22):
                        ps, bps = next_ps()
                        if m < 20:
                            c0, mw = m * 128, 128
                        else:
                            c0, mw = 2560 + (m - 20) * 32, 32
                        for c in range(8):
                            mm(ps[0:mw, :], winb[:, c, c0:c0 + mw], hT[:, c, :], c == 0, c == 7, [b_winb, bhT], [bps])
                        sg, bsg = stg[ns % 4], b_stg[ns % 4]
                        ns += 1
                        cp(ev_eng(), sg[0:mw, :], ps[0:mw, :], [bps], [bsg])
                        S.dma("pool", projT[c0:c0 + mw, tb * 512:(tb + 1) * 512], sg[0:mw, :], reads=[bsg], writes=[b_projT[m][tb]])
                    for i in range(4):
                        ti = tb * 4 + i
                        ps, bps = next_ps()
                        for c in range(8):
                            mm(ps[:], hT[:, c, i * 128:(i + 1) * 128], winb[:, c, NFM:NFM + 512], c == 0, c == 7, [b_winb, bhT], [bps])
                        sg, bsg = stg[ns % 4], b_stg[ns % 4]
                        ns += 1
                        cp(ev_eng(), sg[:], ps[:], [bps], [bsg])
                        S.dma("pool", projTM[ti * 128:(ti + 1) * 128, :], sg[:], reads=[bsg], writes=[b_projTM[ti]])
                for _ in wgen:
                    pass

        def stage_ret(l):
            lg = [math.log(1.0 - 2.0 ** (-5.0 - h)) for h in range(4)]
            with ExitStack() as st:
                dm = [sb(st, [128, 128], F32) for _ in range(4)]; b_tab = Buf()
                qdec = [sb(st, [128, 128], F32) for _ in range(2)]
                kdec = sb(st, [128, 4], F32)
                ii = sb(st, [128, 128], I32)
                ff = sb(st, [128, 128], F32)
                S.op("pool", lambda e: e.iota(ii[:], pattern=[[1, 128]], base=0, channel_multiplier=-1), writes=[b_tab])
                cp("dve", ff[:], ii[:], [b_tab], [b_tab])
                for h in range(4):
                    act(dm[h][:], ff[:], AF.Exp, [b_tab], [b_tab], scale=lg[h])
                    tt("dve", dm[h][:], dm[h][:], m_iu[:], ALU.mult, [b_tab, b_c], [b_tab])
                S.op("pool", lambda e: e.iota(ii[:], pattern=[[1, 128]], base=1, channel_multiplier=0), reads=[b_tab], writes=[b_tab])
                cp("dve", ff[:], ii[:], [b_tab], [b_tab])
                for h in range(4):
                    act(qdec[h // 2][(h % 2) * 64:(h % 2) * 64 + 64, :], ff[(h % 2) * 64:(h % 2) * 64 + 64, :], AF.Exp, [b_tab], [b_tab], scale=lg[h])
                S.op("pool", lambda e: e.iota(ii[:, 0:1], pattern=[[0, 1]], base=127, channel_multiplier=-1), reads=[b_tab], writes=[b_tab])
                cp("dve", ff[:, 0:1], ii[:, 0:1], [b_tab], [b_tab])
                for h in range(4):
                    act(kdec[:, h:h + 1], ff[:, 0:1], AF.Exp, [b_tab], [b_tab], scale=lg[h])
                Rf = [sb(st, [128, 64], F32) for _ in range(2)]; b_Rf = [Buf(), Buf()]
                Rb = [sb(st, [128, 64], BF16) for _ in range(2)]; b_Rb = [Buf(), Buf()]
                for hp in range(2):
                    S.op("pool", lambda e, hp=hp: e.memset(Rf[hp][:], 0.0), writes=[b_Rf[hp]])
                    S.op("pool", lambda e, hp=hp: e.memset(Rb[hp][:], 0.0), writes=[b_Rb[hp]])
                NB = 2
                inq = [[sb(st, [128, 4, 128], F32) for _ in range(2)] for _ in range(NB)]
                b_inq = [[Buf() for _ in range(2)] for _ in range(NB)]
                tab = [sb(st, [128, 2, 128], F32) for _ in range(NB)]; b_tb = [Buf() for _ in range(NB)]
                vg = [sb(st, [128, 512], F32) for _ in range(NB)]; b_vg = [Buf() for _ in range(NB)]
                vb = [sb(st, [128, 256], BF16) for _ in range(NB)]; b_vb = [Buf() for _ in range(NB)]
                gs = [sb(st, [128, 256], F32) for _ in range(NB)]; b_gs = [Buf() for _ in range(NB)]
                t1 = [sb(st, [128, 128], F32) for _ in range(2)]; b_t1 = [Buf(), Buf()]
                t2 = [sb(st, [128, 128], F32) for _ in range(2)]; b_t2 = [Buf(), Buf()]
                qr = [sb(st, [128, 128], BF16) for _ in range(2)]; b_qr = [Buf(), Buf()]
                kr = [sb(st, [128, 128], BF16) for _ in range(2)]; b_kr = [Buf(), Buf()]
                qd = [sb(st, [128, 128], BF16) for _ in range(2)]; b_qd = [Buf(), Buf()]
                ktok = [sb(st, [128, 128], BF16) for _ in range(2)]; b_ktok = [Buf(), Buf()]
                stm = [sb(st, [128, 128], BF16) for _ in range(2)]; b_stm = [Buf(), Buf()]
                ssq = [sb(st, [128, 1], F32) for _ in range(2)]; b_ssq = [Buf(), Buf()]
                jk = sb(st, [128, 64], F32); b_jk = Buf()
                ym = [sb(st, [128, 256], F32) for _ in range(2)]; b_ym = [Buf(), Buf()]
                nh = 0
                for n in range(32):
                    k = n % NB
                    tb = n // 4
                    cs = slice(n * 128, (n + 1) * 128)
                    S.dma("sp", tab[k][:], ropeT[0:2, :, cs].rearrange("a p t -> p a t"), reads=[b_rope[tb]], writes=[b_tb[k]])
                    S.dma("sp", vg[k][:], projTM[cs, :], reads=[b_projTM[n]], writes=[b_vg[k]])
                    cp("pool", vb[k][:], vg[k][:, 0:256], [b_vg[k]], [b_vb[k]])
                    act(gs[k][:], vg[k][:, 256:512], AF.Silu, [b_vg[k]], [b_gs[k]])
                    for hp in range(2):
                        for a, m in enumerate((7 + hp, 9 + hp, 11 + hp, 13 + hp)):
                            S.dma("sp", inq[k][hp][:, a, :], projT[m * 128:(m + 1) * 128, cs], reads=[b_projT[m][tb]], writes=[b_inq[k][hp]])
                    for hp in range(2):
                        I = inq[k][hp]; bI = b_inq[k][hp]
                        tt("dve", t1[hp][:], I[:, 0, :], tab[k][:, 0, :], ALU.mult, [bI, b_tb[k]], [b_t1[hp]])
                        tt("pool", t2[hp][:], I[:, 1, :], tab[k][:, 1, :], ALU.mult, [bI, b_tb[k]], [b_t2[hp]])
                        tt("dve", t1[hp][:], t1[hp][:], t2[hp][:], ALU.add, [b_t1[hp], b_t2[hp]], [b_t1[hp]])
                        cp("act", qr[hp][:], t1[hp][:], [b_t1[hp]], [b_qr[hp]])
                        tt("pool", qd[hp][:], t1[hp][:], qdec[hp][:], ALU.mult, [b_t1[hp], b_tab], [b_qd[hp]])
                        tt("dve", t1[hp][:], I[:, 2, :], tab[k][:, 0, :], ALU.mult, [bI, b_tb[k], b_qr[hp], b_qd[hp]], [b_t1[hp]])
                        tt("pool", t2[hp][:], I[:, 3, :], tab[k][:, 1, :], ALU.mult, [bI, b_tb[k]], [b_t2[hp]])
                        stt(kr[hp][:], t1[hp][:], 1.0, t2[hp][:], ALU.mult, ALU.add, [b_t1[hp], b_t2[hp]], [b_kr[hp]])
                        scl("act", kr[hp][:], kr[hp][:], 0.125, [b_kr[hp]], [b_kr[hp]])
                        pb, bpb = next_pb()
                        tr(pb[:, 0, :], kr[hp][:], ident_b[:], [b_kr[hp], b_c], [bpb])
                        for hh in range(2):
                            h = hp * 2 + hh
                            scl("dve", ktok[hp][:, hh * 64:hh * 64 + 64], pb[:, 0, hh * 64:hh * 64 + 64], kdec[:, h:h + 1],
                                [bpb, b_tab], [b_ktok[hp]])
                        for hh in range(2):
                            h = hp * 2 + hh
                            rows = slice(hh * 64, hh * 64 + 64)
                            ps, bps = next_ps()
                            mm(ps[:, 0:128], kr[hp][rows, :], qr[hp][rows, :], True, True, [b_kr[hp], b_qr[hp]], [bps])
                            j = nh % 2
                            nh += 1
                            tt("dve", stm[j][:], ps[:, 0:128], dm[h][:], ALU.mult, [bps, b_tab], [b_stm[j]])
                            ps2, bps2 = next_ps()
                            mm(ps2[:, 0:64], stm[j][:], vb[k][:, h * 64:(h + 1) * 64], True, False, [b_stm[j], b_vb[k]], [bps2])
                            mm(ps2[:, 0:64], qd[hp][rows, :], Rb[hp][rows, :], False, True, [b_qd[hp], b_Rb[hp]], [bps2])
                            act(jk[:], ps2[:, 0:64], AF.Square, [bps2], [b_jk, b_ssq[j]], accum_out=ssq[j][:, 0:1])
                            rstd_of(ssq[j][:], ssq[j][:], 1.0 / 64, [b_ssq[j]], [b_ssq[j]])
                            stt(ym[k][:, h * 64:(h + 1) * 64], ps2[:, 0:64], ssq[j][:, 0:1], gs[k][:, h * 64:(h + 1) * 64],
                                ALU.mult, ALU.mult, [bps2, b_ssq[j], b_gs[k]], [b_ym[k]])
                            ps3, bps3 = next_ps()
                            mm(ps3[:, 0:64], ktok[hp][:], vb[k][:, h * 64:(h + 1) * 64], True, True, [b_ktok[hp], b_vb[k]], [bps3])
                            stt(Rf[hp][rows, :], Rf[hp][rows, :], math.exp(lg[h] * 128), ps3[rows, 0:64], ALU.mult, ALU.add,
                                [b_Rf[hp], bps3], [b_Rf[hp]])
                            cp("act", Rb[hp][rows, :], Rf[hp][rows, :], [b_Rf[hp]], [b_Rb[hp]])
                    S.dma("pool", mix[cs, 256:512], ym[k][:], reads=[b_ym[k]], writes=[b_mix[n]])


        def stage_mlap(l):
            with ExitStack() as st:
                bw = Buf()
                wqf = sb(st, [128, 3, 1536], F32)
                wqb = sb(st, [128, 3, 1536], BF16)
                qng = sb(st, [128, 3], F32)
                wkf = sb(st, [128, 2, 1024], F32)
                wkb = sb(st, [128, 2, 1024], BF16)
                kvg = sb(st, [128, 2], F32)
                S.dma("sp", qng[:], q_norm[l].rearrange("(c p) -> p c", p=128), writes=[bw], allow_slow_non_contiguous=True)
                S.dma("sp", kvg[:], kv_norm[l].rearrange("(c p) -> p c", p=128), writes=[bw], allow_slow_non_contiguous=True)
                S.dma("sp", wqf[:], wq_p[l].rearrange("(c p) n -> p c n", p=128), writes=[bw])
                S.dma("sp", wkf[:, :, 0:512], wkvk[l].rearrange("(c p) n -> p c n", p=128), writes=[bw])
                S.dma("sp", wkf[:, :, 512:1024], wkvv[l].rearrange("(c p) n -> p c n", p=128), writes=[bw])
                for c in range(3):
                    scl(("act", "dve", "pool")[c], wqb[:, c, :], wqf[:, c, :], qng[:, c:c + 1], [bw], [bw])
                for c in range(2):
                    scl(("act", "dve")[c], wkb[:, c, :], wkf[:, c, :], kvg[:, c:c + 1], [bw], [bw])
                lat = [sb(st, [128, 5, 512], F32) for _ in range(2)]; b_lat = [Buf(), Buf()]
                krr = [sb(st, [32, 2, 512], F32) for _ in range(2)]; b_krr = [Buf(), Buf()]
                tabm = [sb(st, [128, 2, 512], F32) for _ in range(2)]; b_tabm = [Buf(), Buf()]
                sq = sb(st, [128, 5, 512], F32); b_sq = Buf()
                rq = sb(st, [128, 2, 512], F32); b_rq = Buf()
                latn = sb(st, [128, 5, 512], BF16); b_latn = Buf()
                ta = [sb(st, [128, 512], F32) for _ in range(2)]; b_ta = [Buf(), Buf()]
                tb_ = [sb(st, [128, 512], F32) for _ in range(2)]; b_tb2 = [Buf(), Buf()]
                qsb = [sb(st, [96, 512], BF16) for _ in range(2)]; b_qsb = [Buf(), Buf()]
                ksb = [sb(st, [64, 512], BF16) for _ in range(2)]; b_ksb = [Buf(), Buf()]
                krb = sb(st, [32, 512], BF16); b_krb = Buf()
                vsb = [sb(st, [128, 512], BF16) for _ in range(2)]; b_vsb = [Buf(), Buf()]
                bd = Buf()
                for tb in range(8):
                    k = tb % 2
                    cs = slice(tb * 512, (tb + 1) * 512)
                    S.dma("sp", lat[k][:], projT[1920:2560, cs].rearrange("(a p) t -> p a t", p=128), writes=[b_lat[k]])
                    S.dma("sp", krr[k][:], projT[2560:2624, cs].rearrange("(a p) t -> p a t", p=32), writes=[b_krr[k]])
                    S.dma("sp", tabm[k][:], ropeT[2:4, :, cs].rearrange("a p t -> p a t"), writes=[b_tabm[k]])
                    for a in range(5):
                        tt(("pool", "dve")[a % 2], sq[:, a, :], lat[k][:, a, :], lat[k][:, a, :], ALU.mult, [b_lat[k]], [b_sq])
                    ps, bps = next_ps()
                    for a in range(3):
                        mm(ps[:], ones_f[:], sq[:, a, :], a == 0, a == 2, [b_c, b_sq], [bps])
                    rstd_of(rq[:, 0, :], ps[:], 1.0 / 384, [bps], [b_rq])
                    ps, bps = next_ps()
                    for a in range(2):
                        mm(ps[:], ones_f[:], sq[:, 3 + a, :], a == 0, a == 1, [b_c, b_sq], [bps])
                    rstd_of(rq[:, 1, :], ps[:], 1.0 / 256, [bps], [b_rq])
                    for a in range(5):
                        tt(("dve", "pool")[a % 2], latn[:, a, :], lat[k][:, a, :], rq[:, 0 if a < 3 else 1, :], ALU.mult,
                           [b_lat[k], b_rq], [b_latn])
                    for h in range(8):
                        j = h % 2
                        psm, bpsm = next_ps()
                        for c in range(3):
                            mm(psm[0:96, :], wqb[:, c, h * 192:h * 192 + 96], latn[:, c, :], c == 0, c == 2, [bw, b_latn], [bpsm])
                        pss, bpss = next_ps()
                        for c in range(3):
                            mm(pss[0:96, :], wqb[:, c, h * 192 + 96:h * 192 + 192], latn[:, c, :], c == 0, c == 2, [bw, b_latn], [bpss])
                        cp("act", qsb[j][0:64, :], psm[0:64, :], [bpsm], [b_qsb[j]])
                        tt("dve", ta[j][64:96, :], psm[64:96, :], tabm[k][64:96, 0, :], ALU.mult, [bpsm, b_tabm[k]], [b_ta[j]])
                        tt("dve", tb_[j][64:96, :], pss[64:96, :], tabm[k][64:96, 1, :], ALU.mult, [bpss, b_tabm[k]], [b_tb2[j]])
                        tt("pool", qsb[j][64:96, :], ta[j][64:96, :], tb_[j][64:96, :], ALU.add, [b_ta[j], b_tb2[j]], [b_qsb[j]])
                        S.dma("pool", qT[h, :, cs], qsb[j][:], reads=[b_qsb[j]], writes=[bd])
                        psk, bpsk = next_ps()
                        for c in range(2):
                            mm(psk[0:64, :], wkb[:, c, h * 64:(h + 1) * 64], latn[:, 3 + c, :], c == 0, c == 1, [bw, b_latn], [bpsk])
                        cp("act", ksb[j][:], psk[0:64, :], [bpsk], [b_ksb[j]])
                        S.dma("pool", kT[h, 0:64, cs], ksb[j][:], reads=[b_ksb[j]], writes=[bd])
                    tt("dve", ta[0][0:32, :], krr[k][:, 0, :], tabm[k][0:32, 0, :], ALU.mult, [b_krr[k], b_tabm[k]], [b_ta[0]])
                    tt("dve", tb_[0][0:32, :], krr[k][:, 1, :], tabm[k][0:32, 1, :], ALU.mult, [b_krr[k], b_tabm[k]], [b_tb2[0]])
                    tt("pool", krb[:], ta[0][0:32, :], tb_[0][0:32, :], ALU.add, [b_ta[0], b_tb2[0]], [b_krb])
                    for h in range(8):
                        S.dma("sp", kT[h, 64:96, cs], krb[:], reads=[b_krb], writes=[bd])
                    for i in range(4):
                        ti = tb * 4 + i
                        ps, bps = next_ps()
                        for c in range(2):
                            mm(ps[:], latn[:, 3 + c, i * 128:(i + 1) * 128], wkb[:, c, 512:1024], c == 0, c == 1, [bw, b_latn], [bps])
                        cp(ev_eng(), vsb[i % 2][:], ps[:], [bps], [b_vsb[i % 2]])
                        S.dma("pool", vtok[ti * 128:(ti + 1) * 128, :], vsb[i % 2][:], reads=[b_vsb[i % 2]], writes=[bd])

        def stage_mlaa(l):
            sc = 96.0 ** -0.5
            with ExitStack() as st:
                kTs = [sb(st, [96, S_LEN], BF16) for _ in range(2)]; b_kTs = [Buf(), Buf()]
                qTs = [sb(st, [96, S_LEN], BF16) for _ in range(2)]; b_qTs = [Buf(), Buf()]
                vs = [sb(st, [128, 32, 64], BF16) for _ in range(2)]; b_vs = [Buf(), Buf()]
                mixc = sb(st, [128, 32, 512], F32); b_mixc = Buf()
                Ssb = [sb(st, [128, S_LEN], F32) for _ in range(3)]; b_Ssb = [Buf(), Buf(), Buf()]
                Pb = [sb(st, [128, S_LEN], BF16) for _ in range(2)]; b_Pb = [Buf(), Buf()]
                PT = [sb(st, [128, 8, 128], BF16) for _ in range(2)]; b_PT = [Buf(), Buf()]
                mx = [sb(st, [128, 2], F32) for _ in range(3)]; b_mx = [Buf(), Buf(), Buf()]
                rs = [sb(st, [128, 2], F32) for _ in range(2)]; b_rs = [Buf(), Buf()]
                bd = Buf()

                def scores(k, i):
                    nk = i + 1
                    p = i % 3
                    nkc = (nk + 3) // 4
                    for kc in range(nkc):
                        cols = min(512, nk * 128 - kc * 512)
                        ps, bps = next_ps()
                        mm(ps[:, 0:cols], qTs[k][:, i * 128:(i + 1) * 128], kTs[k][:, kc * 512:kc * 512 + cols], True, True,
                           [b_qTs[k], b_kTs[k]], [bps])
                        if kc == nkc - 1:
                            if cols > 128:
                                act(Ssb[p][:, kc * 512:kc * 512 + cols - 128], ps[:, 0:cols - 128], AF.Copy, [bps], [b_Ssb[p]], scale=sc)
                            stt(Ssb[p][:, nk * 128 - 128:nk * 128], ps[:, cols - 128:cols], sc, cmask[:], ALU.mult, ALU.add,
                                [bps, b_c], [b_Ssb[p]])
                        else:
                            act(Ssb[p][:, kc * 512:kc * 512 + 512], ps[:, 0:512], AF.Copy, [bps], [b_Ssb[p]], scale=sc)

                def rowmax(k, i):
                    nk = i + 1
                    p = i % 3
                    S.op("dve", lambda e: e.tensor_reduce(mx[p][:, 0:1], Ssb[p][:, 0:nk * 128], AX.X, ALU.max), reads=[b_Ssb[p]], writes=[b_mx[p]])
                    scl("dve", mx[p][:, 1:2], mx[p][:, 0:1], -1.0, [b_mx[p]], [b_mx[p]])

                def softmax(k, i):
                    nk = i + 1
                    p = i % 3
                    q = i % 2
                    act(Pb[q][:, 0:nk * 128], Ssb[p][:, 0:nk * 128], AF.Exp, [b_Ssb[p], b_mx[p]], [b_Pb[q], b_rs[q]],
                        bias=mx[p][:, 1:2], accum_out=rs[q][:, 0:1])

                def pv(k, h, i):
                    nk = i + 1
                    p = i % 2
                    pso, bpso = next_ps()
                    ng = (nk + 7) // 8
                    for g in range(ng):
                        nb = min(8, nk - g * 8)
                        pb, bpb = next_pb()
                        for j in range(nb):
                            tr(pb[:, j, :], Pb[p][:, (g * 8 + j) * 128:(g * 8 + j + 1) * 128], ident_b[:], [b_Pb[p], b_c], [bpb])
                        cp("dve", PT[g % 2][:, 0:nb, :], pb[:, 0:nb, :], [bpb], [b_PT[g % 2]])
                        for j in range(nb):
                            mm(pso[:, 0:64], PT[g % 2][:, j, :], vs[k][:, g * 8 + j, :], g == 0 and j == 0, g == ng - 1 and j == nb - 1,
                               [b_PT[g % 2], b_vs[k]], [bpso])
                    S.op("dve", lambda e: e.reciprocal(rs[p][:, 1:2], rs[p][:, 0:1]), reads=[b_rs[p]], writes=[b_rs[p]])
                    scl("act", mixc[:, i, h * 64:(h + 1) * 64], pso[:, 0:64], rs[p][:, 1:2], [bpso, b_rs[p]], [b_mixc])

                for h in range(8):
                    k = h % 2
                    S.dma("sp", kTs[k][:], kT[h], writes=[b_kTs[k]])
                    S.dma("sp", qTs[k][:], qT[h], writes=[b_qTs[k]])
                    S.dma("sp", vs[k][:], vtok[:, h * 64:(h + 1) * 64].rearrange("(kt p) v -> p kt v", p=128), writes=[b_vs[k]])
                    scores(k, 0)
                    rowmax(k, 0)
                    scores(k, 1)
                    for i in range(32):
                        softmax(k, i)
                        if i + 1 < 32:
                            rowmax(k, i + 1)
                        if i + 2 < 32:
                            scores(k, i + 2)
                        pv(k, h, i)
                S.dma("pool", mix[:, 512:1024].rearrange("(i p) c -> p i c", p=128), mixc[:], reads=[b_mixc], writes=[bd])

        def wgu_gen(l, st):
            gt = sb(st, [128, 8], F32); b_gt = Buf()
            S.dma("sp", gt[:], ffn_norm[l].rearrange("(c p) -> p c", p=128), writes=[b_gt], allow_slow_non_contiguous=True)
            wst = [sb(st, [128, 8, 512], F32) for _ in range(2)]; b_wst = [Buf(), Buf()]
            wsb = [sb(st, [128, 8, 512], BF16) for _ in range(2)]; b_wsb = [Buf(), Buf()]
            bd = Buf()
            for j in range(11):
                k = j % 2
                S.dma("sp", wst[k][:], w_gu[l, :, j * 512:(j + 1) * 512].rearrange("(c p) n -> p c n", p=128), writes=[b_wst[k]])
                for c in range(8):
                    scl(("act", "pool", "dve")[c % 3], wsb[k][:, c, :], wst[k][:, c, :], gt[:, c:c + 1], [b_wst[k], b_gt], [b_wsb[k]])
                S.dma("pool", wgub[:, j * 512:(j + 1) * 512].rearrange("(c p) n -> p c n", p=128), wsb[k][:], reads=[b_wsb[k]], writes=[bd])
                yield

        def stage_wgu(l):
            with ExitStack() as st:
                for _ in wgu_gen(l, st):
                    pass

        def stage_f(l, last):
            hsrc = x if l == 0 else hbuf
            with ExitStack() as st:
                bw = Buf()
                wob = sb(st, [128, 8, D], BF16)
                wdb = sb(st, [128, 22, D], BF16)
                wst = [sb(st, [128, 8, 512], F32) for _ in range(1)]; b_wst = [Buf()]
                n = 0
                for j in range(2):
                    k = 0; n += 1
                    S.dma("sp", wst[k][:], w_out[l, :, j * 512:(j + 1) * 512].rearrange("(c p) n -> p c n", p=128), writes=[b_wst[k]])
                    for c in range(8):
                        cp(("act", "pool", "dve")[c % 3], wob[:, c, j * 512:(j + 1) * 512], wst[k][:, c, :], [b_wst[k]], [bw])
                for j in range(2):
                    for c0 in range(0, 22, 8):
                        nc_ = min(8, 22 - c0)
                        k = 0; n += 1
                        S.dma("sp", wst[k][:, 0:nc_, :], w_down[l, c0 * 128:(c0 + nc_) * 128, j * 512:(j + 1) * 512].rearrange("(c p) n -> p c n", p=128),
                              writes=[b_wst[k]])
                        for c in range(nc_):
                            cp(("act", "pool", "dve")[c % 3], wdb[:, c0 + c, j * 512:(j + 1) * 512], wst[k][:, c, :], [b_wst[k]], [bw])
                if last:
                    fg = sb(st, [128, D], F32)
                    S.dma("sp", fg[:], final_norm.partition_broadcast(128), writes=[bw])
                hh = [sb(st, [128, 4, D], F32) for _ in range(2)]; b_hh = [Buf(), Buf()]
                mt = [sb(st, [128, D], F32) for _ in range(2)]; b_mt = [Buf(), Buf()]
                mb = [sb(st, [128, D], BF16) for _ in range(2)]; b_mb = [Buf(), Buf()]
                mT = sb(st, [128, 8, 512], BF16); b_mT = Buf()
                junk = sb(st, [128, D], BF16); b_junk = Buf()
                ss = [sb(st, [128, 1], F32) for _ in range(2)]; b_ss = [Buf(), Buf()]
                hT = sb(st, [128, 8, 512], BF16); b_hT = Buf()
                wg = [sb(st, [128, 8, 512], BF16) for _ in range(3)]; b_wg = [Buf(), Buf(), Buf()]
                gsb = [sb(st, [128, 512], F32) for _ in range(2)]; b_gsb = [Buf(), Buf()]
                actT = sb(st, [128, 22, 512], BF16); b_actT = Buf()
                ot = mt; b_ot = b_mt
                bd = Buf()
                nw = 0
                for tb in range(8):
                    H = hh[tb % 2]; bH = b_hh[tb % 2]
                    S.dma("sp", H[:], hsrc[tb * 512:(tb + 1) * 512, :].rearrange("(i p) d -> p i d", p=128), writes=[bH])
                    for i in range(4):
                        ti = tb * 4 + i
                        k = i % 2
                        S.dma("sp", mt[k][:], mix[ti * 128:(ti + 1) * 128, :], writes=[b_mt[k]])
                        cp("pool", mb[k][:], mt[k][:], [b_mt[k]], [b_mb[k]])
                        pb, bpb = next_pb()
                        for c in range(8):
                            tr(pb[:, c, :], mb[k][:, c * 128:(c + 1) * 128], ident_b[:], [b_mb[k], b_c], [bpb])
                        cp(ev_eng(), mT[:, :, i * 128:(i + 1) * 128], pb[:], [bpb], [b_mT])
                    for i in range(4):
                        k = i % 2
                        for j in range(2):
                            ps, bps = next_ps()
                            for c in range(8):
                                mm(ps[:], mT[:, c, i * 128:(i + 1) * 128], wob[:, c, j * 512:(j + 1) * 512], c == 0, c == 7, [b_mT, bw], [bps])
                            tt("dve", H[:, i, j * 512:(j + 1) * 512], H[:, i, j * 512:(j + 1) * 512], ps[:], ALU.add, [bH, bps], [bH])
                        act(junk[:], H[:, i, :], AF.Square, [bH], [b_junk, b_ss[k]], accum_out=ss[k][:, 0:1])
                        rstd_of(ss[k][:], ss[k][:], 1.0 / D, [b_ss[k]], [b_ss[k]])
                        scl("pool", mb[k][:], H[:, i, :], ss[k][:, 0:1], [bH, b_ss[k]], [b_mb[k]])
                        pb, bpb = next_pb()
                        for c in range(8):
                            tr(pb[:, c, :], mb[k][:, c * 128:(c + 1) * 128], ident_b[:], [b_mb[k], b_c], [bpb])
                        cp(ev_eng(), hT[:, :, i * 128:(i + 1) * 128], pb[:], [bpb], [b_hT])
                    for grp in range(11):
                        kw = nw % 3; nw += 1
                        S.dma("sp", wg[kw][:, :, 0:256], wgub[:, grp * 256:(grp + 1) * 256].rearrange("(c p) n -> p c n", p=128), writes=[b_wg[kw]])
                        S.dma("sp", wg[kw][:, :, 256:512], wgub[:, DFF + grp * 256:DFF + (grp + 1) * 256].rearrange("(c p) n -> p c n", p=128),
                              writes=[b_wg[kw]])
                        for u in range(2):
                            f = grp * 2 + u
                            psg, bpsg = next_ps()
                            for c in range(8):
                                mm(psg[:], wg[kw][:, c, u * 128:(u + 1) * 128], hT[:, c, :], c == 0, c == 7, [b_wg[kw], b_hT], [bpsg])
                            psu, bpsu = next_ps()
                            for c in range(8):
                                mm(psu[:], wg[kw][:, c, 256 + u * 128:256 + (u + 1) * 128], hT[:, c, :], c == 0, c == 7, [b_wg[kw], b_hT], [bpsu])
                            act(gsb[u][:], psg[:], AF.Silu, [bpsg], [b_gsb[u]])
                            tt("dve", actT[:, f, :], gsb[u][:], psu[:], ALU.mult, [b_gsb[u], bpsu], [b_actT])
                    for i in range(4):
                        ti = tb * 4 + i
                        k = i % 2
                        for j in range(2):
                            ps, bps = next_ps()
                            for f in range(22):
                                mm(ps[:], actT[:, f, i * 128:(i + 1) * 128], wdb[:, f, j * 512:(j + 1) * 512], f == 0, f == 21, [b_actT, bw], [bps])
                            tt("dve", H[:, i, j * 512:(j + 1) * 512], H[:, i, j * 512:(j + 1) * 512], ps[:], ALU.add, [bH, bps], [bH])
                        if last:
                            act(junk[:], H[:, i, :], AF.Square, [bH], [b_junk, b_ss[k]], accum_out=ss[k][:, 0:1])
                            rstd_of(ss[k][:], ss[k][:], 1.0 / D, [b_ss[k]], [b_ss[k]])
                            stt(ot[k][:], H[:, i, :], ss[k][:, 0:1], fg[:], ALU.mult, ALU.mult, [bH, b_ss[k], bw], [b_ot[k]])
                            S.dma("pool", out[ti * 128:(ti + 1) * 128, :], ot[k][:], reads=[b_ot[k]], writes=[bd])
                    if not last:
                        S.dma("pool", hbuf[tb * 512:(tb + 1) * 512, :].rearrange("(i p) d -> p i d", p=128), H[:], reads=[bH], writes=[bd])


        def stage_rw(l):
            with ExitStack() as st:
                bw = Buf()

                def mk(shape, dt=F32):
                    return sb(st, shape, dt), Buf()

                def col2(src):
                    t = sb(st, [128, 2], F32)
                    S.dma("sp", t[:], src.rearrange("(c p) -> p c", p=128), writes=[bw], allow_slow_non_contiguous=True)
                    return t
                mu = sb(st, [128, 7], F32)
                S.dma("sp", mu[:], rw_mu[l].rearrange("(c p) -> p c", p=128), writes=[bw], allow_slow_non_contiguous=True)
                w0 = col2(rw_w0[l]); a0 = col2(rw_a0[l]); kkc = col2(rw_k_k[l]); kac = col2(rw_k_a[l]); rkc = col2(rw_r_k[l])
                nw0 = sb(st, [128, 2], F32)
                scl("dve", nw0[:], w0[:], -1.0, [bw], [bw])
                if l > 0:
                    v0 = col2(rw_v0[0])
                    v1t = sb(st, [128, 2, 32], F32)
                    S.dma("sp", v1t[:], rw_v1[0].rearrange("(c p) r -> p c r", p=128), writes=[bw])
                    v2t = sb(st, [32, 256], F32)
                    S.dma("sp", v2t[:], rw_v2[0], writes=[bw])
                lowW = sb(st, [64, 256], F32)
                S.dma("sp", lowW[0:32, :], rw_w2[l], writes=[bw])
                S.dma("sp", lowW[32:64, :], rw_a2[l], writes=[bw])
                g2s = sb(st, [128, 2, 64], F32)
                gw = sb(st, [128, 2, 64], F32)
                gb = sb(st, [128, 2, 64], F32)
                for hp in range(2):
                    for hh in range(2):
                        h = hp * 2 + hh
                        S.dma("sp", g2s[hh * 64:(hh + 1) * 64, hp, :], rw_g2[l, :, h * 64:(h + 1) * 64], writes=[bw])
                        S.dma("sp", gw[hh * 64:(hh + 1) * 64, hp, :], rw_gn_w[l, h * 64:(h + 1) * 64].partition_broadcast(64), writes=[bw])
                        S.dma("sp", gb[hh * 64:(hh + 1) * 64, hp, :], rw_gn_b[l, h * 64:(h + 1) * 64].partition_broadcast(64), writes=[bw])
                cm = sb(st, [128, 512], F32)
                S.op("pool", lambda e: e.memset(cm[:], 1.0), writes=[bw])
                S.op("pool", lambda e: e.memset(cm[:].rearrange("p (c t) -> p c t", t=64)[:, :, 0:1], 0.0), reads=[bw], writes=[bw])
                BDn = ("kap", "r", "b", "k", "rk", "v")
                BD = {}
                bBD = {}
                for hp in range(2):
                    for q in BDn:
                        BD[q, hp], bBD[q, hp] = mk([128, 8, 128])
                        S.op("pool", lambda e, t=BD[q, hp]: e.memset(t[:], 0.0), writes=[bBD[q, hp]])
                BDsg, bBDsg = mk([128, 8, 128])
                S.op("pool", lambda e: e.memset(BDsg[:], 0.0), writes=[bBDsg])
                Tst = []
                for hp in range(2):
                    t, b = mk([128, 64])
                    S.op("pool", lambda e, t=t: e.memset(t[:], 0.0), writes=[b])
                    Tst.append((t, b))
                gamC = [mk([128, 8]) for _ in range(2)]
                yseg = [mk([128, 8, 64]) for _ in range(2)]
                raw = [mk([128, 513]) for _ in range(7)]
                sh = [mk([128, 512]) for _ in range(7)]
                dtmp = mk([128, 512])
                sg0 = mk([64, 512])
                tmp = {n: mk([128, 512]) for n in ("e1", "ew", "lw", "a", "kk", "sq", "kkn", "t", "kmod", "b", "lgm", "gam", "gin", "gpr", "rk")}
                vv = mk([32, 512]); sgv = mk([128, 512]); vf = mk([128, 512])
                W = {}
                for hp in range(2):
                    for par in range(4):
                        d = {}
                        for n in ("A0", "A1", "B0", "B1", "Pm0", "Pm1", "Pt0", "Pt1", "AkT", "RBT", "RKT", "BbT", "BkT"):
                            d[n] = mk([128, 128])
                        for n in ("Xs", "Us", "yn", "jk", "Ysb"):
                            d[n] = mk([128, 64])
                        d["VG"] = mk([128, 130])
                        d["Vs"] = (d["VG"][0][:, 0:64], d["VG"][1])
                        d["Gs"] = (d["VG"][0][:, 64:128], d["VG"][1])
                        d["rks"] = (d["VG"][0][:, 128:130], d["VG"][1])
                        d["st"] = mk([128, 8])
                        W[hp, par] = d

                def v3(ap):
                    return ap.rearrange("p (c t) -> p c t", t=64)

                def pre(hp, c, par):
                    d = W[hp, par]
                    kap, r_, b_, k_, rk_, v_ = [BD[q, hp][:, c, :] for q in BDn]
                    bk = [bBD[q, hp] for q in BDn]
                    psA, bA = next_ps()
                    mm(psA[:, 0:128], b_, kap, True, True, [bk[2], bk[0]], [bA])
                    mm(psA[:, 128:256], kap, b_, True, True, [bk[2], bk[0]], [bA])
                    mm(psA[:, 256:384], k_, kap, True, True, [bk[3], bk[0]], [bA])
                    mm(psA[:, 384:512], b_, r_, True, True, [bk[2], bk[1]], [bA])
                    psB, bB = next_ps()
                    mm(psB[:, 0:128], k_, r_, True, True, [bk[3], bk[1]], [bB])
                    tt("dve", d["A0"][0][:], psA[:, 0:128], m_su[:], ALU.mult, [bA, b_c], [d["A0"][1]])
                    tt("dve", d["B0"][0][:], psA[:, 128:256], m_sl[:], ALU.mult, [bA, b_c], [d["B0"][1]])
                    tt("dve", d["AkT"][0][:], psA[:, 256:384], m_su[:], ALU.mult, [bA, b_c], [d["AkT"][1]])
                    tt("dve", d["RBT"][0][:], psA[:, 384:512], m_iu[:], ALU.mult, [bA, b_c], [d["RBT"][1]])
                    tt("dve", d["RKT"][0][:], psB[:, 0:128], m_iu[:], ALU.mult, [bB, b_c], [d["RKT"][1]])
                    psC, bC = next_ps()
                    mm(psC[:, 0:64], v_, istack[:], True, True, [bk[5], b_c], [bC])
                    mm(psC[:, 128:192], BDsg[:, c, :], g2s[:, hp, :], True, True, [bBDsg, bw], [bC])
                    mm(psC[:, 256:258], rk_, ones_f[:, 0:2], True, True, [bk[4], b_c], [bC])
                    cp("dve", d["VG"][0][:, 0:64], psC[:, 0:64], [bC], [d["VG"][1]])
                    cp("dve", d["VG"][0][:, 64:128], psC[:, 128:192], [bC], [d["VG"][1]])
                    cp("dve", d["VG"][0][:, 128:130], psC[:, 256:258], [bC], [d["VG"][1]])
                    tt("pool", d["Pm0"][0][:], ident_f[:], d["A0"][0][:], ALU.subtract, [b_c, d["A0"][1]], [d["Pm0"][1]])
                    tt("pool", d["Pt0"][0][:], ident_f[:], d["B0"][0][:], ALU.subtract, [b_c, d["B0"][1]], [d["Pt0"][1]])
                    yield
                    import os
                    pstop = int(os.environ.get("PRE_STOP", "9"))
                    if pstop <= 1:
                        return
                    psT, bT = next_ps()
                    tr(psT[:, 0:128], b_, ident_f[:], [bk[2], b_c], [bT])
                    tr(psT[:, 128:256], k_, ident_f[:], [bk[3], b_c], [bT])
                    cp("dve", d["BbT"][0][:], psT[:, 0:128], [bT], [d["BbT"][1]])
                    cp("dve", d["BkT"][0][:], psT[:, 128:256], [bT], [d["BkT"][1]])
                    yield
                    if pstop <= 2:
                        return
                    for j in range(5):
                        A, bAj = d["A%d" % (j % 2)]
                        B, bBj = d["B%d" % (j % 2)]
                        An, bAn = d["A%d" % ((j + 1) % 2)]
                        Bn, bBn = d["B%d" % ((j + 1) % 2)]
                        Pm, bPm = d["Pm%d" % (j % 2)]
                        Pt, bPt = d["Pt%d" % (j % 2)]
                        Pmn, bPmn = d["Pm%d" % ((j + 1) % 2)]
                        Ptn, bPtn = d["Pt%d" % ((j + 1) % 2)]
                        ps, bps = next_ps()
                        mm(ps[:, 0:128], B[:], A[:], True, True, [bBj, bAj], [bps])
                        if j < 4:
                            mm(ps[:, 128:256], A[:], B[:], True, True, [bBj, bAj], [bps])
                        cp("dve", An[:], ps[:, 0:128], [bps], [bAn])
                        if j < 4:
                            cp("dve", Bn[:], ps[:, 128:256], [bps], [bBn])
                        yield
                        ps2, bps2 = next_ps()
                        mm(ps2[:, 0:128], Pt[:], An[:], True, False, [bPt, bAn], [bps2])
                        mm(ps2[:, 0:128], Pt[:], ident_f[:], False, True, [bPt, b_c], [bps2])
                        if j < 4:
                            mm(ps2[:, 128:256], An[:], Pt[:], True, False, [bPt, bAn], [bps2])
                            mm(ps2[:, 128:256], ident_f[:], Pt[:], False, True, [bPt, b_c], [bps2])
                        cp("dve", Pmn[:], ps2[:, 0:128], [bps2], [bPmn])
                        if j < 4:
                            cp("dve", Ptn[:], ps2[:, 128:256], [bps2], [bPtn])
                        yield

                def seq(hp, c, par, seg):
                    d = W[hp, par]
                    kap, r_, b_, k_, rk_, v_ = [BD[q, hp][:, c, :] for q in BDn]
                    bk = [bBD[q, hp] for q in BDn]
                    T, bT_ = Tst[hp]
                    MT, bMT = d["Pm1"]
                    psX, bX = next_ps()
                    mm(psX[:, 0:64], kap, T[:], True, False, [bk[0], bT_], [bX])
                    mm(psX[:, 0:64], d["AkT"][0][:], d["Vs"][0], False, True, [d["AkT"][1], d["Vs"][1]], [bX])
                    cp("dve", d["Xs"][0][:], psX[:, 0:64], [bX], [d["Xs"][1]])
                    yield
                    psU, bU = next_ps()
                    mm(psU[:, 0:64], MT[:], d["Xs"][0][:], True, True, [bMT, d["Xs"][1]], [bU])
                    scl("dve", d["Us"][0][:], psU[:, 0:64], -1.0, [bU], [d["Us"][1]])
                    yield
                    psY, bY = next_ps()
                    mm(psY[:, 0:64], r_, T[:], True, False, [bk[1], bT_], [bY])
                    mm(psY[:, 0:64], d["RBT"][0][:], d["Us"][0][:], False, False, [d["RBT"][1], d["Us"][1]], [bY])
                    mm(psY[:, 0:64], d["RKT"][0][:], d["Vs"][0], False, True, [d["RKT"][1], d["Vs"][1]], [bY])
                    mm(psY[:, 128:192], d["BbT"][0][:], d["Us"][0][:], True, False, [d["BbT"][1], d["Us"][1]], [bY])
                    mm(psY[:, 128:192], d["BkT"][0][:], d["Vs"][0], False, False, [d["BkT"][1], d["Vs"][1]], [bY])
                    mm(psY[:, 128:192], ident_f[:], T[:], False, True, [b_c, bT_], [bY])
                    gC, bgC = gamC[hp]
                    scl("dve", T[:], psY[:, 128:192], gC[:, c:c + 1], [bY, bgC], [bT_])
                    stt_, bst = d["st"]
                    jk, bjk = d["jk"]
                    yn, byn = d["yn"]
                    Ysb, bYsb = d["Ysb"]
                    cp("dve", Ysb[:], psY[:, 0:64], [bY], [bYsb])
                    S.op("dve", lambda e: e.tensor_reduce(stt_[:, 0:1], Ysb[:], AX.X, ALU.add), reads=[bYsb], writes=[bst])
                    act(jk[:], Ysb[:], AF.Square, [bYsb], [bjk, bst], accum_out=stt_[:, 1:2])
                    scl("dve", stt_[:, 2:3], stt_[:, 0:1], 1.0 / 64, [bst], [bst])
                    tt("dve", stt_[:, 3:4], stt_[:, 2:3], stt_[:, 2:3], ALU.mult, [bst], [bst])
                    stt(stt_[:, 4:5], stt_[:, 1:2], 1.0 / 64, stt_[:, 3:4], ALU.mult, ALU.subtract, [bst], [bst])
                    rstd_of(stt_[:, 5:6], stt_[:, 4:5], 1.0, [bst], [bst], eps=64e-5)
                    S.op("dve", lambda e: e.tensor_scalar(yn[:], Ysb[:], stt_[:, 2:3], stt_[:, 5:6], ALU.subtract, ALU.mult),
                         reads=[bYsb, bst], writes=[byn])
                    tt("pool", yn[:], yn[:], gw[:, hp, :], ALU.mult, [byn, bw], [byn])
                    tt("pool", yn[:], yn[:], gb[:, hp, :], ALU.add, [byn, bw], [byn])
                    stt(yn[:], d["Vs"][0], d["rks"][0][:, 0:1], yn[:], ALU.mult, ALU.add, [d["Vs"][1], d["rks"][1], byn], [byn])
                    ys, bys = yseg[hp]
                    tt("pool", ys[:, c, :], yn[:], d["Gs"][0], ALU.mult, [byn, d["Gs"][1]], [bys])
                    if c == 7:
                        for hh in range(2):
                            h = hp * 2 + hh
                            S.dma("pool", mix[seg * 512:(seg + 1) * 512, h * 64:(h + 1) * 64].rearrange("(c t) v -> t c v", t=64),
                                  ys[hh * 64:(hh + 1) * 64, :, :], reads=[bys], writes=[bw])
                    yield

                def run(gens):
                    gens = list(gens)
                    while gens:
                        for g in list(gens):
                            try:
                                next(g)
                            except StopIteration:
                                gens.remove(g)

                import os
                for seg in range(int(os.environ.get('RW_SEGS', '8'))):
                    cs0 = seg * 512
                    cs = slice(cs0, cs0 + 512)
                    for m in range(7):
                        R_, bR = raw[m]
                        if seg == 0:
                            S.op("pool", lambda e, R_=R_: e.memset(R_[:, 0:1], 0.0), writes=[bR])
                            S.dma("sp", R_[:, 1:513], projT[m * 128:(m + 1) * 128, 0:512], writes=[bR])
                        else:
                            S.dma("sp", R_[:, :], projT[m * 128:(m + 1) * 128, cs0 - 1:cs0 + 512], writes=[bR])
                        tt(("dve", "pool")[m % 2], dtmp[0][:], R_[:, 0:512], R_[:, 1:513], ALU.subtract, [bR], [dtmp[1]])
                        stt(sh[m][0][:], dtmp[0][:], mu[:, m:m + 1], R_[:, 1:513], ALU.mult, ALU.add, [dtmp[1], bR, bw], [sh[m][1]])
                    lr, blr = sh[6]
                    act(lr[0:32, :], lr[0:32, :], AF.Tanh, [blr], [blr])
                    act(lr[64:128, :], lr[64:128, :], AF.Sigmoid, [blr], [blr])
                    S.dma("sp", sg0[0][:], lr[64:128, :], reads=[blr], writes=[sg0[1]])
                    cp("pool", BDsg[0:64, :, 0:64], v3(sg0[0][:]), [sg0[1]], [bBDsg])
                    cp("pool", BDsg[64:128, :, 64:128], v3(lr[64:128, :]), [blr], [bBDsg])
                    if l > 0:
                        ps, bps = next_ps()
                        for c in range(2):
                            mm(ps[0:32, :], v1t[:, c, :], sh[4 + c][0][:], c == 0, c == 1, [bw, sh[4 + c][1]], [bps])
                        cp("dve", vv[0][:], ps[0:32, :], [bps], [vv[1]])
                    for hp in range(2):
                        r, br = sh[hp]
                        k, bk_ = sh[2 + hp]
                        v, bv = sh[4 + hp]
                        if l > 0:
                            ps, bps = next_ps()
                            mm(ps[:], v2t[:, hp * 128:(hp + 1) * 128], vv[0][:], True, True, [bw, vv[1]], [bps])
                            act(sgv[0][:], ps[:], AF.Sigmoid, [bps, bw], [sgv[1]], bias=v0[:, hp:hp + 1])
                            S.dma("sp", vf[0][:], vfirst[hp * 128:(hp + 1) * 128, cs], writes=[vf[1]])
                            tt("dve", vf[0][:], vf[0][:], v[:], ALU.subtract, [vf[1], bv], [vf[1]])
                            tt("dve", vf[0][:], vf[0][:], sgv[0][:], ALU.mult, [vf[1], sgv[1]], [vf[1]])
                            tt("dve", v[:], v[:], vf[0][:], ALU.add, [vf[1], bv], [bv])
                        else:
                            S.dma("pool", vfirst[hp * 128:(hp + 1) * 128, cs], v[:], reads=[bv], writes=[bw])
                        e1, be1 = tmp["e1"]; ew, bew = tmp["ew"]; lw, blw = tmp["lw"]; a_, ba = tmp["a"]
                        kk, bkk = tmp["kk"]; sq, bsq = tmp["sq"]; kkn, bkkn = tmp["kkn"]; t_, bt_ = tmp["t"]
                        kmod, bkm = tmp["kmod"]; bb, bbb = tmp["b"]; lgm, blg = tmp["lgm"]; gam, bgam = tmp["gam"]
                        gin, bgin = tmp["gin"]; gpr, bgpr = tmp["gpr"]; rk, brk = tmp["rk"]
                        ps, bps = next_ps()
                        mm(ps[:], lowW[0:32, hp * 128:(hp + 1) * 128], lr[0:32, :], True, True, [bw, blr], [bps])
                        act(e1[:], ps[:], AF.Exp, [bps, bw], [be1], scale=-1.0, bias=nw0[:, hp:hp + 1])
                        act(e1[:], e1[:], AF.Ln, [be1], [be1], bias=1.0)
                        act(ew[:], e1[:], AF.Exp, [be1], [bew], scale=-1.0, bias=-0.5)
                        scl("pool", lw[:], ew[:], -1.0, [bew], [blw])
                        ps, bps = next_ps()
                        mm(ps[:], lowW[32:64, hp * 128:(hp + 1) * 128], lr[32:64, :], True, True, [bw, blr], [bps])
                        act(a_[:], ps[:], AF.Sigmoid, [bps, bw], [ba], bias=a0[:, hp:hp + 1])
                        scl("pool", kk[:], k[:], kkc[:, hp:hp + 1], [bk_, bw], [bkk])
                        tt("pool", sq[:], kk[:], kk[:], ALU.mult, [bkk], [bsq])
                        ps, bps = next_ps()
                        mm(ps[:], ones_bd[:], sq[:], True, True, [b_c, bsq], [bps])
                        S.op("dve", lambda e, ps=ps, sq=sq: e.tensor_scalar(sq[:], ps[:], 1e-24, None, ALU.max), reads=[bps], writes=[bsq])
                        act(sq[:], sq[:], AF.Ln, [bsq], [bsq])
                        act(sq[:], sq[:], AF.Exp, [bsq], [bsq], scale=-0.5)
                        tt("dve", kkn[:], kk[:], sq[:], ALU.mult, [bkk, bsq], [bkkn])
                        S.op("dve", lambda e, t_=t_, a_=a_, hp=hp: e.tensor_scalar(t_[:], a_[:], -1.0, kac[:, hp:hp + 1], ALU.add, ALU.mult),
                             reads=[ba, bw], writes=[bt_])
                        stt(kmod[:], t_[:], 1.0, k[:], ALU.add, ALU.mult, [bt_, bk_], [bkm])
                        tt("pool", bb[:], kkn[:], a_[:], ALU.mult, [bkkn, ba], [bbb])
                        S.op("dve", lambda e, lgm=lgm, lw=lw: e.tensor_tensor_scan(lgm[:], cm[:], lw[:], 0.0, ALU.mult, ALU.add),
                             reads=[bw, blw], writes=[blg])
                        act(gam[:], lgm[:], AF.Exp, [blg], [bgam])
                        act(gin[:], lgm[:], AF.Exp, [blg], [bgin], scale=-1.0)
                        tt("pool", gpr[:], lgm[:], lw[:], ALU.subtract, [blg, blw], [bgpr])
                        act(gpr[:], gpr[:], AF.Exp, [bgpr], [bgpr])
                        tt("pool", rk[:], r[:], kmod[:], ALU.mult, [br, bkm], [brk])
                        for hh in range(2):
                            rows = slice(hh * 64, hh * 64 + 64)
                            cb = slice(hh * 64, hh * 64 + 64)
                            tt("dve", BD["kap", hp][rows, :, cb], v3(kkn[rows, :]), v3(gpr[rows, :]), ALU.mult, [bkkn, bgpr], [bBD["kap", hp]])
                            tt("pool", BD["r", hp][rows, :, cb], v3(r[rows, :]), v3(gam[rows, :]), ALU.mult, [br, bgam], [bBD["r", hp]])
                            tt("dve", BD["b", hp][rows, :, cb], v3(bb[rows, :]), v3(gin[rows, :]), ALU.mult, [bbb, bgin], [bBD["b", hp]])
                            tt("pool", BD["k", hp][rows, :, cb], v3(kmod[rows, :]), v3(gin[rows, :]), ALU.mult, [bkm, bgin], [bBD["k", hp]])
                            scl("dve", BD["rk", hp][rows, :, cb], v3(rk[rows, :]), rkc[rows, hp:hp + 1], [brk, bw], [bBD["rk", hp]])
                            cp("act", BD["v", hp][rows, :, cb], v3(v[rows, :]), [bv], [bBD["v", hp]])
                        cp("pool", gamC[hp][0][:], v3(gam[:])[:, :, 63], [bgam], [gamC[hp][1]])
                    import os
                    rwm = os.environ.get("RW_MODE", "full")
                    if rwm == "prep":
                        continue
                    def seqchain(hp, cs_):
                        for c_ in cs_:
                            yield from seq(hp, c_, c_ % 4, seg)

                    if rwm != "full":
                        for c in range(8):
                            run([pre(0, c, c % 4), pre(1, c, c % 4)])
                        continue
                    for pc in range(4):
                        c0 = 2 * pc
                        gl = [pre(0, c0, c0 % 4), pre(1, c0, c0 % 4), pre(0, c0 + 1, (c0 + 1) % 4), pre(1, c0 + 1, (c0 + 1) % 4)]
                        if pc > 0:
                            gl += [seqchain(0, (c0 - 2, c0 - 1)), seqchain(1, (c0 - 2, c0 - 1))]
                        run(gl)
                    run([seqchain(0, (6, 7)), seqchain(1, (6, 7))])

        if stages is None:
            stages = ["rope"]
            for l_ in range(L):
                stages += ["p1w_%d" % l_, "ret_%d" % l_, "mlap_%d" % l_, "mlaa_%d" % l_, "rw_%d" % l_, "f_%d" % l_]
        for s_ in stages:
            if s_ == "rope":
                stage_rope()
            elif s_.startswith("p1w_"):
                stage_p1(int(s_[4:]), fuse_wgu=True)
            elif s_.startswith("p1_"):
                stage_p1(int(s_[3:]))
            elif s_.startswith("ret_"):
                stage_ret(int(s_[4:]))
            elif s_.startswith("mlap_"):
                stage_mlap(int(s_[5:]))
            elif s_.startswith("mlaa_"):
                stage_mlaa(int(s_[5:]))
            elif s_.startswith("rw_"):
                stage_rw(int(s_[3:]))
            elif s_.startswith("wgu_"):
                stage_wgu(int(s_[4:]))
            elif s_.startswith("f_"):
                stage_f(int(s_[2:]), int(s_[2:]) == L - 1)
            S.barrier()
        final = [b for bl in (b_out, b_mix, b_hbuf, b_projTM, b_rope) for b in bl] + [b for bl in b_projT for b in bl]
        S.finish(final)
    return nc


def _prep_inputs(inp):
    w_in = np.asarray(inp["w_in"])
    cols = list(range(896))
    rb = 896
    q = [rb + i for i in range(256)]
    kk = [rb + 256 + i for i in range(256)]

    def swap64(c):
        o = []
        for h in range(4):
            o += c[h * 64 + 32:h * 64 + 64] + c[h * 64:h * 64 + 32]
        return o
    cols += q + swap64(q) + kk + swap64(kk)
    mb = 1920
    cols += [mb + i for i in range(640)]
    kr = [mb + 640 + i for i in range(32)]
    cols += kr + kr[16:] + kr[:16]
    assert len(cols) == NFM
    cols += [rb + 512 + i for i in range(512)]
    w_in_p = np.ascontiguousarray(w_in[:, :, cols])
    wq = np.asarray(inp["mla_w_q_up"])
    qc = []
    for h in range(8):
        base = h * 96
        nope = [base + i for i in range(64)]
        rope = [base + 64 + i for i in range(32)]
        qc += nope + rope + nope + rope[16:] + rope[:16]
    wq_p = np.ascontiguousarray(wq[:, :, qc])
    wkv = np.asarray(inp["mla_w_kv_up"])
    kc = [h * 128 + i for h in range(8) for i in range(64)]
    vc = [h * 128 + 64 + i for h in range(8) for i in range(64)]
    r = np.arange(128)
    cst = np.stack([
        (10000.0 ** (-((r % 32) * 2).astype(np.float32) / 64)).astype(np.float32),
        np.where((r % 64) < 32, -1.0, 1.0).astype(np.float32),
        (10000.0 ** (-((r % 16) * 2).astype(np.float32) / 32)).astype(np.float32),
        np.where((r % 32) < 16, -1.0, 1.0).astype(np.float32)], axis=1).astype(np.float32)
    f = lambda k: np.ascontiguousarray(np.asarray(inp[k], dtype=np.float32))
    shared = {
        "cst": cst, "attn_norm": f("attn_norm"), "w_in_p": w_in_p, "w_out": f("w_out"),
        "rw_mu": f("rw_mu"), "rw_w0": f("rw_w0"), "rw_w2": f("rw_w2"), "rw_a0": f("rw_a0"), "rw_a2": f("rw_a2"),
        "rw_g2": f("rw_g2"), "rw_k_k": f("rw_k_k"), "rw_k_a": f("rw_k_a"), "rw_r_k": f("rw_r_k").reshape(L, 256),
        "rw_gn_w": f("rw_gn_w"), "rw_gn_b": f("rw_gn_b"), "rw_v0": f("rw_v0"), "rw_v1": f("rw_v1"), "rw_v2": f("rw_v2"),
        "q_norm": f("mla_q_norm"), "kv_norm": f("mla_kv_norm"), "wq_p": wq_p,
        "wkvk": np.ascontiguousarray(wkv[:, :, kc]), "wkvv": np.ascontiguousarray(wkv[:, :, vc]),
        "ffn_norm": f("ffn_norm"), "w_gu": f("w_gate_up"), "w_down": f("w_down"), "final_norm": f("final_norm"),
    }
    xs = np.asarray(inp["x"], dtype=np.float32)
    ps = np.asarray(inp["positions"]).astype(np.int32)
    maps = []
    for c in range(8):
        b = c % 4
        m = dict(shared)
        m["x"] = np.ascontiguousarray(xs[b])
        m["pos"] = np.ascontiguousarray(ps[b])
        maps.append(m)
    return maps


def kernel(**inputs):
    maps = _prep_inputs(inputs)
    nc = build()
    res = run_bass_kernel_spmd(nc, maps, core_ids=list(range(8)))
    return np.stack([res.results[b]["out"] for b in range(4)], axis=0).astype(np.float32)
```

```python
import math
import numpy as np
from contextlib import ExitStack
import concourse.bass as bass
import concourse.mybir as mybir
from concourse.bass_utils import run_bass_kernel_spmd

F32 = mybir.dt.float32
BF16 = mybir.dt.bfloat16
I32 = mybir.dt.int32
AF = mybir.ActivationFunctionType
ALU = mybir.AluOpType
AX = mybir.AxisListType

ENGS = ("pe", "act", "dve", "pool", "sp")
NDSEM = 6

S_LEN = 4096
D = 1024
L = 2
NFM = 20 * 128 + 64
NW = NFM + 512
DFF = 2816
EPS = 1e-6


class Buf:
    __slots__ = ("w", "r")

    def __init__(self):
        self.w = None
        self.r = {}


class Sched:
    def __init__(self, nc, es):
        self.nc = nc
        self.ops = {e: [] for e in ENGS}
        self.cnt = {e: 0 for e in ENGS}
        self.sems = {}
        for e in ENGS:
            self.sems[e] = es.enter_context(nc.semaphore("c_" + e))
        self.dcnt = {}
        for e in ("sp", "pool", "act"):
            for i in range(NDSEM):
                k = "d_%s%d" % (e, i)
                self.sems[k] = es.enter_context(nc.semaphore(k))
                self.dcnt[k] = 0
        self.dnext = {"sp": 0, "pool": 0, "act": 0}
        self.waited = {e: {} for e in ENGS}

    def _waits(self, eng, deps):
        for (k, v) in sorted(deps):
            if eng == "pe" and k == "pe":
                continue
            if self.waited[eng].get(k, 0) < v:
                self.waited[eng][k] = v
                sem = self.sems[k]
                self.ops[eng].append(lambda e, sem=sem, v=v: e.wait_ge(sem, v))

    @staticmethod
    def _deps(reads, writes):
        deps = set()
        for b in reads:
            if b.w is not None:
                deps.add(b.w)
        for b in writes:
            if b.w is not None:
                deps.add(b.w)
            deps.update(b.r.values())
        return deps

    def op(self, eng, fn, reads=(), writes=()):
        self._waits(eng, self._deps(reads, writes))
        self.cnt[eng] += 1
        dep = (eng, self.cnt[eng])
        sem = self.sems[eng]
        self.ops[eng].append(lambda e, fn=fn, sem=sem: fn(e).then_inc(sem, 1))
        for b in reads:
            b.r[eng] = dep
        for b in writes:
            b.w = dep
            b.r = {}

    def dma(self, q, out, in_, reads=(), writes=(), **kw):
        i = self.dnext[q]
        self.dnext[q] = (i + 1) % NDSEM
        k = "d_%s%d" % (q, i)
        deps = self._deps(reads, writes)
        if self.dcnt[k] > 0:
            deps.add((k, 16 * self.dcnt[k]))
        self._waits(q, deps)
        self.dcnt[k] += 1
        dep = (k, 16 * self.dcnt[k])
        sem = self.sems[k]
        self.ops[q].append(
            lambda e, out=out, in_=in_, sem=sem, kw=kw: e.dma_start(out=out, in_=in_, **kw).then_inc(sem, 16))
        for b in reads:
            b.r["dma" + k] = dep
        for b in writes:
            b.w = dep
            b.r = {}

    def barrier(self):
        deps = set()
        for e in ENGS:
            if self.cnt[e] > 0:
                deps.add((e, self.cnt[e]))
        for k, c in self.dcnt.items():
            if c > 0:
                deps.add((k, 16 * c))
        for e in ENGS:
            self._waits(e, {d for d in deps if d[0] != e})

    def finish(self, out_bufs):
        deps = set()
        for b in out_bufs:
            if b.w is not None:
                deps.add(b.w)
        self._waits("sp", deps)
        ops = self.ops
        with self.nc.Block() as block:
            @block.tensor
            def _(e):
                for f in ops["pe"]:
                    f(e)

            @block.scalar
            def _(e):
                for f in ops["act"]:
                    f(e)

            @block.vector
            def _(e):
                for f in ops["dve"]:
                    f(e)

            @block.gpsimd
            def _(e):
                for f in ops["pool"]:
                    f(e)

            @block.sync
            def _(e):
                for f in ops["sp"]:
                    f(e)


def build(stages=None, dbg=()):
    nc = bass.Bass("TRN2", target_bir_lowering=False)

    def din(name, shape, dt=F32):
        return nc.dram_tensor(name, list(shape), dt, kind="ExternalInput").ap()

    def dscr(name, shape, dt=F32):
        kind = "ExternalOutput" if name in dbg else "Internal"
        return nc.dram_tensor(name, list(shape), dt, kind=kind).ap()

    x = din("x", [S_LEN, D])
    pos = din("pos", [S_LEN], I32)
    cst = din("cst", [128, 4])
    attn_norm = din("attn_norm", [L, D])
    w_in_p = din("w_in_p", [L, D, NW])
    w_out = din("w_out", [L, D, D])
    rw_mu = din("rw_mu", [L, 896])
    rw_w0 = din("rw_w0", [L, 256])
    rw_w2 = din("rw_w2", [L, 32, 256])
    rw_a0 = din("rw_a0", [L, 256])
    rw_a2 = din("rw_a2", [L, 32, 256])
    rw_g2 = din("rw_g2", [L, 64, 256])
    rw_k_k = din("rw_k_k", [L, 256])
    rw_k_a = din("rw_k_a", [L, 256])
    rw_r_k = din("rw_r_k", [L, 256])
    rw_gn_w = din("rw_gn_w", [L, 256])
    rw_gn_b = din("rw_gn_b", [L, 256])
    rw_v0 = din("rw_v0", [1, 256])
    rw_v1 = din("rw_v1", [1, 256, 32])
    rw_v2 = din("rw_v2", [1, 32, 256])
    q_norm = din("q_norm", [L, 384])
    kv_norm = din("kv_norm", [L, 256])
    wq_p = din("wq_p", [L, 384, 8 * 192])
    wkvk = din("wkvk", [L, 256, 512])
    wkvv = din("wkvv", [L, 256, 512])
    ffn_norm = din("ffn_norm", [L, D])
    w_gu = din("w_gu", [L, D, 2 * DFF])
    w_down = din("w_down", [L, DFF, D])
    final_norm = din("final_norm", [D])
    out = nc.dram_tensor("out", [S_LEN, D], F32, kind="ExternalOutput").ap()

    projT = dscr("projT", [NFM, S_LEN])
    projTM = dscr("projTM", [S_LEN, 512])
    ropeT = dscr("ropeT", [4, 128, S_LEN])
    qT = dscr("qT", [8, 96, S_LEN], BF16)
    kT = dscr("kT", [8, 96, S_LEN], BF16)
    vtok = dscr("vtok", [S_LEN, 512], BF16)
    mix = dscr("mix", [S_LEN, D])
    hbuf = dscr("hbuf", [S_LEN, D])
    wgub = dscr("wgub", [D, 2 * DFF], BF16)
    vfirst = dscr("vfirst", [256, S_LEN])

    b_projT = [[Buf() for _ in range(8)] for _ in range(22)]
    b_projTM = [Buf() for _ in range(32)]
    b_rope = [Buf() for _ in range(8)]
    b_qT = [Buf() for _ in range(8)]
    b_kT = [Buf() for _ in range(8)]
    b_vtok = Buf()
    b_mix = [Buf() for _ in range(32)]
    b_hbuf = [Buf() for _ in range(32)]
    b_wgub = Buf()
    b_vfirst = [Buf() for _ in range(8)]
    b_out = [Buf() for _ in range(32)]
    b_none = [Buf() for _ in range(32)]

    es = ExitStack()
    with es:
        S = Sched(nc, es)
        uid = [0]

        def sb(st, shape, dt, name=None):
            uid[0] += 1
            return st.enter_context(nc.sbuf_tensor("%s_%d" % (name or "t", uid[0]), list(shape), dt))

        PS = [es.enter_context(nc.psum_tensor("ps%d" % i, [128, 512], F32)) for i in range(6)]
        b_PS = [Buf() for _ in range(6)]
        PB = [es.enter_context(nc.psum_tensor("pb%d" % i, [128, 8, 128], BF16)) for i in range(2)]
        b_PB = [Buf() for _ in range(2)]
        rr = {"ps": 0, "pb": 0, "ev": 0}

        def next_ps():
            i = rr["ps"]
            rr["ps"] = (i + 1) % 6
            return PS[i], b_PS[i]

        def next_pb():
            i = rr["pb"]
            rr["pb"] = (i + 1) % 2
            return PB[i], b_PB[i]

        def ev_eng():
            rr["ev"] ^= 1
            return "act" if rr["ev"] else "dve"

        def mm(o, lt, rh, st_, sp_, R, W):
            S.op("pe", lambda e: e.matmul(o, lt, rh, start=st_, stop=sp_), reads=R, writes=W)

        def tr(o, i_, idt, R, W):
            S.op("pe", lambda e: e.transpose(o, i_, idt), reads=R, writes=W)

        def cp(eng, o, i_, R, W):
            if eng == "act":
                S.op("act", lambda e: e.copy(o, i_), reads=R, writes=W)
            else:
                S.op(eng, lambda e: e.tensor_copy(o, i_), reads=R, writes=W)

        def scl(eng, o, i_, sc, R, W):
            if eng == "act":
                S.op("act", lambda e: e.activation(o, i_, AF.Copy, scale=sc), reads=R, writes=W)
            else:
                S.op(eng, lambda e: e.tensor_scalar(o, i_, sc, None, ALU.mult), reads=R, writes=W)

        def tt(eng, o, a, b, op, R, W):
            S.op(eng, lambda e: e.tensor_tensor(o, a, b, op), reads=R, writes=W)

        def stt(o, a, sc, b, op0, op1, R, W, eng="dve"):
            S.op(eng, lambda e: e.scalar_tensor_tensor(o, a, sc, b, op0, op1), reads=R, writes=W)

        def act(o, i_, func, R, W, **kw):
            S.op("act", lambda e: e.activation(o, i_, func, **kw), reads=R, writes=W)

        def rstd_of(o, i_, scale, R, W, eps=EPS):
            act(o, i_, AF.Ln, R, W, scale=scale, bias=eps)
            act(o, o, AF.Exp, W, W, scale=-0.5)

        ident_f = sb(es, [128, 128], F32, "identf"); b_c = Buf()
        ident_b = sb(es, [128, 128], BF16, "identb")
        m_su = sb(es, [128, 128], F32, "msu")
        m_iu = sb(es, [128, 128], F32, "miu")
        m_sl = sb(es, [128, 128], F32, "msl")
        ones_f = sb(es, [128, 128], F32, "onesf")
        ones_bd = sb(es, [128, 128], F32, "onesbd")
        istack = sb(es, [128, 64], F32, "istack")
        cmask = sb(es, [128, 128], F32, "cmask")
        cstt = sb(es, [128, 4], F32, "cstt")
        S.dma("sp", cstt[:], cst[:, :], writes=[b_c])
        S.op("pool", lambda e: e.memset(ones_f[:], 1.0), writes=[b_c])
        S.op("pool", lambda e: e.memset(ones_bd[:], 0.0), writes=[b_c])
        S.op("pool", lambda e: e.memset(ones_bd[0:64, 0:64], 1.0), writes=[b_c])
        S.op("pool", lambda e: e.memset(ones_bd[64:128, 64:128], 1.0), writes=[b_c])
        S.op("pool", lambda e: e.memset(ident_f[:], 0.0), writes=[b_c])
        S.op("pool", lambda e: e.affine_select(ident_f[:], ident_f[:], pattern=[[-1, 128]], compare_op=ALU.not_equal,
                                               fill=1.0, base=0, channel_multiplier=1), reads=[b_c], writes=[b_c])
        cp("dve", ident_b[:], ident_f[:], [b_c], [b_c])
        S.op("pool", lambda e: e.affine_select(m_su[:], ones_f[:], pattern=[[1, 128]], compare_op=ALU.is_gt,
                                               fill=0.0, base=0, channel_multiplier=-1), reads=[b_c], writes=[b_c])
        S.op("pool", lambda e: e.affine_select(m_iu[:], ones_f[:], pattern=[[1, 128]], compare_op=ALU.is_ge,
                                               fill=0.0, base=0, channel_multiplier=-1), reads=[b_c], writes=[b_c])
        S.op("pool", lambda e: e.affine_select(m_sl[:], ones_f[:], pattern=[[-1, 128]], compare_op=ALU.is_gt,
                                               fill=0.0, base=0, channel_multiplier=1), reads=[b_c], writes=[b_c])
        S.op("pool", lambda e: e.memset(cmask[:], 0.0), writes=[b_c])
        S.op("pool", lambda e: e.affine_select(cmask[:], cmask[:], pattern=[[-1, 128]], compare_op=ALU.is_ge,
                                               fill=-1e30, base=0, channel_multiplier=1), reads=[b_c], writes=[b_c])
        cp("dve", istack[0:64, :], ident_f[0:64, 0:64], [b_c], [b_c])
        cp("dve", istack[64:128, :], ident_f[64:128, 64:128], [b_c], [b_c])

        def stage_rope():
            with ExitStack() as st:
                posi = sb(st, [128, 512], I32); b_posi = Buf()
                posf = sb(st, [128, 512], F32); b_posf = Buf()
                tt_ = [sb(st, [128, 512], F32) for _ in range(2)]; b_tt = [Buf(), Buf()]
                ki = sb(st, [128, 512], I32); b_ki = Buf()
                kf = sb(st, [128, 512], F32); b_kf = Buf()
                ot = [sb(st, [128, 512], F32) for _ in range(2)]; b_ot = [Buf(), Buf()]
                n = 0
                for tb in range(8):
                    S.dma("sp", posi[:], pos[tb * 512:(tb + 1) * 512].partition_broadcast(128), writes=[b_posi])
                    cp("dve", posf[:], posi[:], [b_posi], [b_posf])
                    for ti in range(4):
                        fcol = 0 if ti < 2 else 2
                        is_sin = ti % 2 == 1
                        t_, bt = tt_[n % 2], b_tt[n % 2]
                        o_, bo = ot[n % 2], b_ot[n % 2]
                        n += 1
                        S.op("dve", lambda e, t_=t_, fcol=fcol: e.tensor_scalar(t_[:], posf[:], cstt[:, fcol:fcol + 1], None, ALU.mult),
                             reads=[b_posf, b_c], writes=[bt])
                        off = 0.5 if is_sin else 0.75
                        S.op("dve", lambda e, t_=t_, off=off: e.tensor_scalar(t_[:], t_[:], 1.0 / (2 * math.pi), off, ALU.mult, ALU.add),
                             reads=[bt], writes=[bt])
                        cp("dve", ki[:], t_[:], [bt], [b_ki])
                        cp("dve", kf[:], ki[:], [b_ki], [b_kf])
                        tt("dve", t_[:], t_[:], kf[:], ALU.subtract, [bt, b_kf], [bt])
                        stt(t_[:], t_[:], 0.0, t_[:], ALU.is_lt, ALU.add, [bt], [bt])
                        S.op("dve", lambda e, t_=t_: e.tensor_scalar(t_[:], t_[:], 2 * math.pi, -math.pi, ALU.mult, ALU.add),
                             reads=[bt], writes=[bt])
                        act(o_[:], t_[:], AF.Sin, [bt], [bo])
                        if is_sin:
                            scl("pool", o_[:], o_[:], cstt[:, fcol + 1:fcol + 2], [bo, b_c], [bo])
                        S.dma("pool", ropeT[ti, :, tb * 512:(tb + 1) * 512], o_[:], reads=[bo], writes=[b_rope[tb]])

        def stage_p1(l, fuse_wgu=False):
            hsrc = x if l == 0 else hbuf
            b_hsrc = b_none if l == 0 else b_hbuf
            with ExitStack() as st:
                winb = sb(st, [128, 8, NW], BF16, "winb"); b_winb = Buf()
                gt = sb(st, [128, 8], F32); b_gt = Buf()
                S.dma("sp", gt[:], attn_norm[l].rearrange("(c p) -> p c", p=128), writes=[b_gt], allow_slow_non_contiguous=True)
                wst = [sb(st, [128, 8, 512], F32) for _ in range(2)]; b_wst = [Buf(), Buf()]
                for j in range(7):
                    c0 = j * 512
                    cw = min(512, NW - c0)
                    k = j % 2
                    S.dma("sp", wst[k][:, :, 0:cw], w_in_p[l, :, c0:c0 + cw].rearrange("(c p) n -> p c n", p=128), writes=[b_wst[k]])
                    for c in range(8):
                        scl(("act", "pool", "dve")[c % 3], winb[:, c, c0:c0 + cw], wst[k][:, c, 0:cw], gt[:, c:c + 1],
                            [b_wst[k], b_gt], [b_winb])
                ht = [sb(st, [128, D], F32) for _ in range(2)]; b_ht = [Buf(), Buf()]
                junk = sb(st, [128, D], BF16); b_junk = Buf()
                ss = [sb(st, [128, 1], F32) for _ in range(2)]; b_ss = [Buf(), Buf()]
                hnb = [sb(st, [128, D], BF16) for _ in range(2)]; b_hnb = [Buf(), Buf()]
                hnT = [sb(st, [128, 8, 512], BF16) for _ in range(2)]; b_hnT = [Buf(), Buf()]
                stg = [sb(st, [128, 512], F32) for _ in range(4)]; b_stg = [Buf() for _ in range(4)]
                ns = 0
                wgen = wgu_gen(l, st) if fuse_wgu else iter(())
                for tb in range(8):
                    for _ in range(2 if tb < 3 else 1):
                        next(wgen, None)
                    hT, bhT = hnT[tb % 2], b_hnT[tb % 2]
                    for i in range(4):
                        ti = tb * 4 + i
                        k = ti % 2
                        S.dma("sp", ht[k][:], hsrc[ti * 128:(ti + 1) * 128, :], reads=[b_hsrc[ti]], writes=[b_ht[k]])
                        act(junk[:], ht[k][:], AF.Square, [b_ht[k]], [b_junk, b_ss[k]], accum_out=ss[k][:, 0:1])
                        rstd_of(ss[k][:], ss[k][:], 1.0 / D, [b_ss[k]], [b_ss[k]])
                        scl("dve", hnb[k][:], ht[k][:], ss[k][:, 0:1], [b_ht[k], b_ss[k]], [b_hnb[k]])
                        pb, bpb = next_pb()
                        for c in range(8):
                            tr(pb[:, c, :], hnb[k][:, c * 128:(c + 1) * 128], ident_b[:], [b_hnb[k], b_c], [bpb])
                        cp(ev_eng(), hT[:, :, i * 128:(i + 1) * 128], pb[:], [bpb], [bhT])
                    for m in range(22):
                        ps, bps = next_ps()
                        if m < 20:
                            c0, mw = m * 128, 128
                        else:
                            c0, mw = 2560 + (m - 20) * 32, 32
                        for c in range(8):
                            mm(ps[0:mw, :], winb[:, c, c0:c0 + mw], hT[:, c, :], c == 0, c == 7, [b_winb, bhT], [bps])
                        sg, bsg = stg[ns % 4], b_stg[ns % 4]
                        ns += 1
                        cp(ev_eng(), sg[0:mw, :], ps[0:mw, :], [bps], [bsg])
                        S.dma("pool", projT[c0:c0 + mw, tb * 512:(tb + 1) * 512], sg[0:mw, :], reads=[bsg], writes=[b_projT[m][tb]])
                    for i in range(4):
                        ti = tb * 4 + i
                        ps, bps = next_ps()
                        for c in range(8):
                            mm(ps[:], hT[:, c, i * 128:(i + 1) * 128], winb[:, c, NFM:NFM + 512], c == 0, c == 7, [b_winb, bhT], [bps])
                        sg, bsg = stg[ns % 4], b_stg[ns % 4]
                        ns += 1
                        cp(ev_eng(), sg[:], ps[:], [bps], [bsg])
                        S.dma("pool", projTM[ti * 128:(ti + 1) * 128, :], sg[:], reads=[bsg], writes=[b_projTM[ti]])
                for _ in wgen:
                    pass

        def stage_ret(l):
            lg = [math.log(1.0 - 2.0 ** (-5.0 - h)) for h in range(4)]
            with ExitStack() as st:
                dm = [sb(st, [128, 128], F32) for _ in range(4)]; b_tab = Buf()
                qdec = [sb(st, [128, 128], F32) for _ in range(2)]
                kdec = sb(st, [128, 4], F32)
                ii = sb(st, [128, 128], I32)
                ff = sb(st, [128, 128], F32)
                S.op("pool", lambda e: e.iota(ii[:], pattern=[[1, 128]], base=0, channel_multiplier=-1), writes=[b_tab])
                cp("dve", ff[:], ii[:], [b_tab], [b_tab])
                for h in range(4):
                    act(dm[h][:], ff[:], AF.Exp, [b_tab], [b_tab], scale=lg[h])
                    tt("dve", dm[h][:], dm[h][:], m_iu[:], ALU.mult, [b_tab, b_c], [b_tab])
                S.op("pool", lambda e: e.iota(ii[:], pattern=[[1, 128]], base=1, channel_multiplier=0), reads=[b_tab], writes=[b_tab])
                cp("dve", ff[:], ii[:], [b_tab], [b_tab])
                for h in range(4):
                    act(qdec[h // 2][(h % 2) * 64:(h % 2) * 64 + 64, :], ff[(h % 2) * 64:(h % 2) * 64 + 64, :], AF.Exp, [b_tab], [b_tab], scale=lg[h])
                S.op("pool", lambda e: e.iota(ii[:, 0:1], pattern=[[0, 1]], base=127, channel_multiplier=-1), reads=[b_tab], writes=[b_tab])
                cp("dve", ff[:, 0:1], ii[:, 0:1], [b_tab], [b_tab])
                for h in range(4):
                    act(kdec[:, h:h + 1], ff[:, 0:1], AF.Exp, [b_tab], [b_tab], scale=lg[h])
                Rf = [sb(st, [128, 64], F32) for _ in range(2)]; b_Rf = [Buf(), Buf()]
                Rb = [sb(st, [128, 64], BF16) for _ in range(2)]; b_Rb = [Buf(), Buf()]
                for hp in range(2):
                    S.op("pool", lambda e, hp=hp: e.memset(Rf[hp][:], 0.0), writes=[b_Rf[hp]])
                    S.op("pool", lambda e, hp=hp: e.memset(Rb[hp][:], 0.0), writes=[b_Rb[hp]])
                NB = 2
                inq = [[sb(st, [128, 4, 128], F32) for _ in range(2)] for _ in range(NB)]
                b_inq = [[Buf() for _ in range(2)] for _ in range(NB)]
                tab = [sb(st, [128, 2, 128], F32) for _ in range(NB)]; b_tb = [Buf() for _ in range(NB)]
                vg = [sb(st, [128, 512], F32) for _ in range(NB)]; b_vg = [Buf() for _ in range(NB)]
                vb = [sb(st, [128, 256], BF16) for _ in range(NB)]; b_vb = [Buf() for _ in range(NB)]
                gs = [sb(st, [128, 256], F32) for _ in range(NB)]; b_gs = [Buf() for _ in range(NB)]
                t1 = [sb(st, [128, 128], F32) for _ in range(2)]; b_t1 = [Buf(), Buf()]
                t2 = [sb(st, [128, 128], F32) for _ in range(2)]; b_t2 = [Buf(), Buf()]
                qr = [sb(st, [128, 128], BF16) for _ in range(2)]; b_qr = [Buf(), Buf()]
                kr = [sb(st, [128, 128], BF16) for _ in range(2)]; b_kr = [Buf(), Buf()]
                qd = [sb(st, [128, 128], BF16) for _ in range(2)]; b_qd = [Buf(), Buf()]
                ktok = [sb(st, [128, 128], BF16) for _ in range(2)]; b_ktok = [Buf(), Buf()]
                stm = [sb(st, [128, 128], BF16) for _ in range(2)]; b_stm = [Buf(), Buf()]
                ssq = [sb(st, [128, 1], F32) for _ in range(2)]; b_ssq = [Buf(), Buf()]
                jk = sb(st, [128, 64], F32); b_jk = Buf()
                ym = [sb(st, [128, 256], F32) for _ in range(2)]; b_ym = [Buf(), Buf()]
                nh = 0
                for n in range(32):
                    k = n % NB
                    tb = n // 4
                    cs = slice(n * 128, (n + 1) * 128)
                    S.dma("sp", tab[k][:], ropeT[0:2, :, cs].rearrange("a p t -> p a t"), reads=[b_rope[tb]], writes=[b_tb[k]])
                    S.dma("sp", vg[k][:], projTM[cs, :], reads=[b_projTM[n]], writes=[b_vg[k]])
                    cp("pool", vb[k][:], vg[k][:, 0:256], [b_vg[k]], [b_vb[k]])
                    act(gs[k][:], vg[k][:, 256:512], AF.Silu, [b_vg[k]], [b_gs[k]])
                    for hp in range(2):
                        for a, m in enumerate((7 + hp, 9 + hp, 11 + hp, 13 + hp)):
                            S.dma("sp", inq[k][hp][:, a, :], projT[m * 128:(m + 1) * 128, cs], reads=[b_projT[m][tb]], writes=[b_inq[k][hp]])
                    for hp in range(2):
                        I = inq[k][hp]; bI = b_inq[k][hp]
                        tt("dve", t1[hp][:], I[:, 0, :], tab[k][:, 0, :], ALU.mult, [bI, b_tb[k]], [b_t1[hp]])
                        tt("pool", t2[hp][:], I[:, 1, :], tab[k][:, 1, :], ALU.mult, [bI, b_tb[k]], [b_t2[hp]])
                        tt("dve", t1[hp][:], t1[hp][:], t2[hp][:], ALU.add, [b_t1[hp], b_t2[hp]], [b_t1[hp]])
                        cp("act", qr[hp][:], t1[hp][:], [b_t1[hp]], [b_qr[hp]])
                        tt("pool", qd[hp][:], t1[hp][:], qdec[hp][:], ALU.mult, [b_t1[hp], b_tab], [b_qd[hp]])
                        tt("dve", t1[hp][:], I[:, 2, :], tab[k][:, 0, :], ALU.mult, [bI, b_tb[k], b_qr[hp], b_qd[hp]], [b_t1[hp]])
                        tt("pool", t2[hp][:], I[:, 3, :], tab[k][:, 1, :], ALU.mult, [bI, b_tb[k]], [b_t2[hp]])
                        stt(kr[hp][:], t1[hp][:], 1.0, t2[hp][:], ALU.mult, ALU.add, [b_t1[hp], b_t2[hp]], [b_kr[hp]])
                        scl("act", kr[hp][:], kr[hp][:], 0.125, [b_kr[hp]], [b_kr[hp]])
                        pb, bpb = next_pb()
                        tr(pb[:, 0, :], kr[hp][:], ident_b[:], [b_kr[hp], b_c], [bpb])
                        for hh in range(2):
                            h = hp * 2 + hh
                            scl("dve", ktok[hp][:, hh * 64:hh * 64 + 64], pb[:, 0, hh * 64:hh * 64 + 64], kdec[:, h:h + 1],
                                [bpb, b_tab], [b_ktok[hp]])
                        for hh in range(2):
                            h = hp * 2 + hh
                            rows = slice(hh * 64, hh * 64 + 64)
                            ps, bps = next_ps()
                            mm(ps[:, 0:128], kr[hp][rows, :], qr[hp][rows, :], True, True, [b_kr[hp], b_qr[hp]], [bps])
                            j = nh % 2
                            nh += 1
                            tt("dve", stm[j][:], ps[:, 0:128], dm[h][:], ALU.mult, [bps, b_tab], [b_stm[j]])
                            ps2, bps2 = next_ps()
                            mm(ps2[:, 0:64], stm[j][:], vb[k][:, h * 64:(h + 1) * 64], True, False, [b_stm[j], b_vb[k]], [bps2])
                            mm(ps2[:, 0:64], qd[hp][rows, :], Rb[hp][rows, :], False, True, [b_qd[hp], b_Rb[hp]], [bps2])
                            act(jk[:], ps2[:, 0:64], AF.Square, [bps2], [b_jk, b_ssq[j]], accum_out=ssq[j][:, 0:1])
                            rstd_of(ssq[j][:], ssq[j][:], 1.0 / 64, [b_ssq[j]], [b_ssq[j]])
                            stt(ym[k][:, h * 64:(h + 1) * 64], ps2[:, 0:64], ssq[j][:, 0:1], gs[k][:, h * 64:(h + 1) * 64],
                                ALU.mult, ALU.mult, [bps2, b_ssq[j], b_gs[k]], [b_ym[k]])
                            ps3, bps3 = next_ps()
                            mm(ps3[:, 0:64], ktok[hp][:], vb[k][:, h * 64:(h + 1) * 64], True, True, [b_ktok[hp], b_vb[k]], [bps3])
                            stt(Rf[hp][rows, :], Rf[hp][rows, :], math.exp(lg[h] * 128), ps3[rows, 0:64], ALU.mult, ALU.add,
                                [b_Rf[hp], bps3], [b_Rf[hp]])
                            cp("act", Rb[hp][rows, :], Rf[hp][rows, :], [b_Rf[hp]], [b_Rb[hp]])
                    S.dma("pool", mix[cs, 256:512], ym[k][:], reads=[b_ym[k]], writes=[b_mix[n]])


        def stage_mlap(l):
            with ExitStack() as st:
                bw = Buf()
                wqf = sb(st, [128, 3, 1536], F32)
                wqb = sb(st, [128, 3, 1536], BF16)
                qng = sb(st, [128, 3], F32)
                wkf = sb(st, [128, 2, 1024], F32)
                wkb = sb(st, [128, 2, 1024], BF16)
                kvg = sb(st, [128, 2], F32)
                S.dma("sp", qng[:], q_norm[l].rearrange("(c p) -> p c", p=128), writes=[bw], allow_slow_non_contiguous=True)
                S.dma("sp", kvg[:], kv_norm[l].rearrange("(c p) -> p c", p=128), writes=[bw], allow_slow_non_contiguous=True)
                S.dma("sp", wqf[:], wq_p[l].rearrange("(c p) n -> p c n", p=128), writes=[bw])
                S.dma("sp", wkf[:, :, 0:512], wkvk[l].rearrange("(c p) n -> p c n", p=128), writes=[bw])
                S.dma("sp", wkf[:, :, 512:1024], wkvv[l].rearrange("(c p) n -> p c n", p=128), writes=[bw])
                for c in range(3):
                    scl(("act", "dve", "pool")[c], wqb[:, c, :], wqf[:, c, :], qng[:, c:c + 1], [bw], [bw])
                for c in range(2):
                    scl(("act", "dve")[c], wkb[:, c, :], wkf[:, c, :], kvg[:, c:c + 1], [bw], [bw])
                lat = [sb(st, [128, 5, 512], F32) for _ in range(2)]; b_lat = [Buf(), Buf()]
                krr = [sb(st, [32, 2, 512], F32) for _ in range(2)]; b_krr = [Buf(), Buf()]
                tabm = [sb(st, [128, 2, 512], F32) for _ in range(2)]; b_tabm = [Buf(), Buf()]
                sq = sb(st, [128, 5, 512], F32); b_sq = Buf()
                rq = sb(st, [128, 2, 512], F32); b_rq = Buf()
                latn = sb(st, [128, 5, 512], BF16); b_latn = Buf()
                ta = [sb(st, [128, 512], F32) for _ in range(2)]; b_ta = [Buf(), Buf()]
                tb_ = [sb(st, [128, 512], F32) for _ in range(2)]; b_tb2 = [Buf(), Buf()]
                qsb = [sb(st, [96, 512], BF16) for _ in range(2)]; b_qsb = [Buf(), Buf()]
                ksb = [sb(st, [64, 512], BF16) for _ in range(2)]; b_ksb = [Buf(), Buf()]
                krb = sb(st, [32, 512], BF16); b_krb = Buf()
                vsb = [sb(st, [128, 512], BF16) for _ in range(2)]; b_vsb = [Buf(), Buf()]
                bd = Buf()
                for tb in range(8):
                    k = tb % 2
                    cs = slice(tb * 512, (tb + 1) * 512)
                    S.dma("sp", lat[k][:], projT[1920:2560, cs].rearrange("(a p) t -> p a t", p=128), writes=[b_lat[k]])
                    S.dma("sp", krr[k][:], projT[2560:2624, cs].rearrange("(a p) t -> p a t", p=32), writes=[b_krr[k]])
                    S.dma("sp", tabm[k][:], ropeT[2:4, :, cs].rearrange("a p t -> p a t"), writes=[b_tabm[k]])
                    for a in range(5):
                        tt(("pool", "dve")[a % 2], sq[:, a, :], lat[k][:, a, :], lat[k][:, a, :], ALU.mult, [b_lat[k]], [b_sq])
                    ps, bps = next_ps()
                    for a in range(3):
                        mm(ps[:], ones_f[:], sq[:, a, :], a == 0, a == 2, [b_c, b_sq], [bps])
                    rstd_of(rq[:, 0, :], ps[:], 1.0 / 384, [bps], [b_rq])
                    ps, bps = next_ps()
                    for a in range(2):
                        mm(ps[:], ones_f[:], sq[:, 3 + a, :], a == 0, a == 1, [b_c, b_sq], [bps])
                    rstd_of(rq[:, 1, :], ps[:], 1.0 / 256, [bps], [b_rq])
                    for a in range(5):
                        tt(("dve", "pool")[a % 2], latn[:, a, :], lat[k][:, a, :], rq[:, 0 if a < 3 else 1, :], ALU.mult,
                           [b_lat[k], b_rq], [b_latn])
                    for h in range(8):
                        j = h % 2
                        psm, bpsm = next_ps()
                        for c in range(3):
                            mm(psm[0:96, :], wqb[:, c, h * 192:h * 192 + 96], latn[:, c, :], c == 0, c == 2, [bw, b_latn], [bpsm])
                        pss, bpss = next_ps()
                        for c in range(3):
                            mm(pss[0:96, :], wqb[:, c, h * 192 + 96:h * 192 + 192], latn[:, c, :], c == 0, c == 2, [bw, b_latn], [bpss])
                        cp("act", qsb[j][0:64, :], psm[0:64, :], [bpsm], [b_qsb[j]])
                        tt("dve", ta[j][64:96, :], psm[64:96, :], tabm[k][64:96, 0, :], ALU.mult, [bpsm, b_tabm[k]], [b_ta[j]])
                        tt("dve", tb_[j][64:96, :], pss[64:96, :], tabm[k][64:96, 1, :], ALU.mult, [bpss, b_tabm[k]], [b_tb2[j]])
                        tt("pool", qsb[j][64:96, :], ta[j][64:96, :], tb_[j][64:96, :], ALU.add, [b_ta[j], b_tb2[j]], [b_qsb[j]])
                        S.dma("pool", qT[h, :, cs], qsb[j][:], reads=[b_qsb[j]], writes=[bd])
                        psk, bpsk = next_ps()
                        for c in range(2):
                            mm(psk[0:64, :], wkb[:, c, h * 64:(h + 1) * 64], latn[:, 3 + c, :], c == 0, c == 1, [bw, b_latn], [bpsk])
                        cp("act", ksb[j][:], psk[0:64, :], [bpsk], [b_ksb[j]])
                        S.dma("pool", kT[h, 0:64, cs], ksb[j][:], reads=[b_ksb[j]], writes=[bd])
                    tt("dve", ta[0][0:32, :], krr[k][:, 0, :], tabm[k][0:32, 0, :], ALU.mult, [b_krr[k], b_tabm[k]], [b_ta[0]])
                    tt("dve", tb_[0][0:32, :], krr[k][:, 1, :], tabm[k][0:32, 1, :], ALU.mult, [b_krr[k], b_tabm[k]], [b_tb2[0]])
                    tt("pool", krb[:], ta[0][0:32, :], tb_[0][0:32, :], ALU.add, [b_ta[0], b_tb2[0]], [b_krb])
                    for h in range(8):
                        S.dma("sp", kT[h, 64:96, cs], krb[:], reads=[b_krb], writes=[bd])
                    for i in range(4):
                        ti = tb * 4 + i
                        ps, bps = next_ps()
                        for c in range(2):
                            mm(ps[:], latn[:, 3 + c, i * 128:(i + 1) * 128], wkb[:, c, 512:1024], c == 0, c == 1, [bw, b_latn], [bps])
                        cp(ev_eng(), vsb[i % 2][:], ps[:], [bps], [b_vsb[i % 2]])
                        S.dma("pool", vtok[ti * 128:(ti + 1) * 128, :], vsb[i % 2][:], reads=[b_vsb[i % 2]], writes=[bd])

        def stage_mlaa(l):
            sc = 96.0 ** -0.5
            with ExitStack() as st:
                kTs = [sb(st, [96, S_LEN], BF16) for _ in range(2)]; b_kTs = [Buf(), Buf()]
                qTs = [sb(st, [96, S_LEN], BF16) for _ in range(2)]; b_qTs = [Buf(), Buf()]
                vs = [sb(st, [128, 32, 64], BF16) for _ in range(2)]; b_vs = [Buf(), Buf()]
                mixc = sb(st, [128, 32, 512], F32); b_mixc = Buf()
                Ssb = [sb(st, [128, S_LEN], F32) for _ in range(3)]; b_Ssb = [Buf(), Buf(), Buf()]
                Pb = [sb(st, [128, S_LEN], BF16) for _ in range(2)]; b_Pb = [Buf(), Buf()]
                PT = [sb(st, [128, 8, 128], BF16) for _ in range(2)]; b_PT = [Buf(), Buf()]
                mx = [sb(st, [128, 2], F32) for _ in range(3)]; b_mx = [Buf(), Buf(), Buf()]
                rs = [sb(st, [128, 2], F32) for _ in range(2)]; b_rs = [Buf(), Buf()]
                bd = Buf()

                def scores(k, i):
                    nk = i + 1
                    p = i % 3
                    nkc = (nk + 3) // 4
                    for kc in range(nkc):
                        cols = min(512, nk * 128 - kc * 512)
                        ps, bps = next_ps()
                        mm(ps[:, 0:cols], qTs[k][:, i * 128:(i + 1) * 128], kTs[k][:, kc * 512:kc * 512 + cols], True, True,
                           [b_qTs[k], b_kTs[k]], [bps])
                        if kc == nkc - 1:
                            if cols > 128:
                                act(Ssb[p][:, kc * 512:kc * 512 + cols - 128], ps[:, 0:cols - 128], AF.Copy, [bps], [b_Ssb[p]], scale=sc)
                            stt(Ssb[p][:, nk * 128 - 128:nk * 128], ps[:, cols - 128:cols], sc, cmask[:], ALU.mult, ALU.add,
                                [bps, b_c], [b_Ssb[p]])
                        else:
                            act(Ssb[p][:, kc * 512:kc * 512 + 512], ps[:, 0:512], AF.Copy, [bps], [b_Ssb[p]], scale=sc)

                def rowmax(k, i):
                    nk = i + 1
                    p = i % 3
                    S.op("dve", lambda e: e.tensor_reduce(mx[p][:, 0:1], Ssb[p][:, 0:nk * 128], AX.X, ALU.max), reads=[b_Ssb[p]], writes=[b_mx[p]])
                    scl("dve", mx[p][:, 1:2], mx[p][:, 0:1], -1.0, [b_mx[p]], [b_mx[p]])

                def softmax(k, i):
                    nk = i + 1
                    p = i % 3
                    q = i % 2
                    act(Pb[q][:, 0:nk * 128], Ssb[p][:, 0:nk * 128], AF.Exp, [b_Ssb[p], b_mx[p]], [b_Pb[q], b_rs[q]],
                        bias=mx[p][:, 1:2], accum_out=rs[q][:, 0:1])

                def pv(k, h, i):
                    nk = i + 1
                    p = i % 2
                    pso, bpso = next_ps()
                    ng = (nk + 7) // 8

                    def tr_grp(g):
                        nb = min(8, nk - g * 8)
                        pb, bpb = next_pb()
                        for j in range(nb):
                            tr(pb[:, j, :], Pb[p][:, (g * 8 + j) * 128:(g * 8 + j + 1) * 128], ident_b[:], [b_Pb[p], b_c], [bpb])
                        cp(ev_eng(), PT[g % 2][:, 0:nb, :], pb[:, 0:nb, :], [bpb], [b_PT[g % 2]])

                    tr_grp(0)
                    for g in range(ng):
                        nb = min(8, nk - g * 8)
                        if g + 1 < ng:
                            tr_grp(g + 1)
                        for j in range(nb):
                            mm(pso[:, 0:64], PT[g % 2][:, j, :], vs[k][:, g * 8 + j, :], g == 0 and j == 0, g == ng - 1 and j == nb - 1,
                               [b_PT[g % 2], b_vs[k]], [bpso])
                    S.op("dve", lambda e: e.reciprocal(rs[p][:, 1:2], rs[p][:, 0:1]), reads=[b_rs[p]], writes=[b_rs[p]])
                    scl("act", mixc[:, i, h * 64:(h + 1) * 64], pso[:, 0:64], rs[p][:, 1:2], [bpso, b_rs[p]], [b_mixc])

                for h in range(8):
                    k = h % 2
                    S.dma("sp", kTs[k][:], kT[h], writes=[b_kTs[k]])
                    S.dma("sp", qTs[k][:], qT[h], writes=[b_qTs[k]])
                    S.dma("sp", vs[k][:], vtok[:, h * 64:(h + 1) * 64].rearrange("(kt p) v -> p kt v", p=128), writes=[b_vs[k]])
                    scores(k, 0)
                    rowmax(k, 0)
                    scores(k, 1)
                    for i in range(32):
                        softmax(k, i)
                        if i + 1 < 32:
                            rowmax(k, i + 1)
                        if i + 2 < 32:
                            scores(k, i + 2)
                        pv(k, h, i)
                S.dma("pool", mix[:, 512:1024].rearrange("(i p) c -> p i c", p=128), mixc[:], reads=[b_mixc], writes=[bd])

        def wgu_gen(l, st):
            gt = sb(st, [128, 8], F32); b_gt = Buf()
            S.dma("sp", gt[:], ffn_norm[l].rearrange("(c p) -> p c", p=128), writes=[b_gt], allow_slow_non_contiguous=True)
            wst = [sb(st, [128, 8, 512], F32) for _ in range(2)]; b_wst = [Buf(), Buf()]
            wsb = [sb(st, [128, 8, 512], BF16) for _ in range(2)]; b_wsb = [Buf(), Buf()]
            bd = Buf()
            for j in range(11):
                k = j % 2
                S.dma("sp", wst[k][:], w_gu[l, :, j * 512:(j + 1) * 512].rearrange("(c p) n -> p c n", p=128), writes=[b_wst[k]])
                for c in range(8):
                    scl(("act", "pool", "dve")[c % 3], wsb[k][:, c, :], wst[k][:, c, :], gt[:, c:c + 1], [b_wst[k], b_gt], [b_wsb[k]])
                S.dma("pool", wgub[:, j * 512:(j + 1) * 512].rearrange("(c p) n -> p c n", p=128), wsb[k][:], reads=[b_wsb[k]], writes=[bd])
                yield

        def stage_wgu(l):
            with ExitStack() as st:
                for _ in wgu_gen(l, st):
                    pass

        def stage_f(l, last):
            hsrc = x if l == 0 else hbuf
            with ExitStack() as st:
                bw = Buf()
                wob = sb(st, [128, 8, D], BF16)
                wdb = sb(st, [128, 22, D], BF16)
                wst = [sb(st, [128, 8, 512], F32) for _ in range(1)]; b_wst = [Buf()]
                n = 0
                for j in range(2):
                    k = 0; n += 1
                    S.dma("sp", wst[k][:], w_out[l, :, j * 512:(j + 1) * 512].rearrange("(c p) n -> p c n", p=128), writes=[b_wst[k]])
                    for c in range(8):
                        cp(("act", "pool", "dve")[c % 3], wob[:, c, j * 512:(j + 1) * 512], wst[k][:, c, :], [b_wst[k]], [bw])
                for j in range(2):
                    for c0 in range(0, 22, 8):
                        nc_ = min(8, 22 - c0)
                        k = 0; n += 1
                        S.dma("sp", wst[k][:, 0:nc_, :], w_down[l, c0 * 128:(c0 + nc_) * 128, j * 512:(j + 1) * 512].rearrange("(c p) n -> p c n", p=128),
                              writes=[b_wst[k]])
                        for c in range(nc_):
                            cp(("act", "pool", "dve")[c % 3], wdb[:, c0 + c, j * 512:(j + 1) * 512], wst[k][:, c, :], [b_wst[k]], [bw])
                if last:
                    fg = sb(st, [128, D], F32)
                    S.dma("sp", fg[:], final_norm.partition_broadcast(128), writes=[bw])
                hh = [sb(st, [128, 4, D], F32) for _ in range(2)]; b_hh = [Buf(), Buf()]
                mt = [sb(st, [128, D], F32) for _ in range(2)]; b_mt = [Buf(), Buf()]
                mb = [sb(st, [128, D], BF16) for _ in range(2)]; b_mb = [Buf(), Buf()]
                mT = sb(st, [128, 8, 512], BF16); b_mT = Buf()
                junk = sb(st, [128, D], BF16); b_junk = Buf()
                ss = [sb(st, [128, 1], F32) for _ in range(2)]; b_ss = [Buf(), Buf()]
                hT = sb(st, [128, 8, 512], BF16); b_hT = Buf()
                wg = [sb(st, [128, 8, 512], BF16) for _ in range(3)]; b_wg = [Buf(), Buf(), Buf()]
                gsb = [sb(st, [128, 512], F32) for _ in range(2)]; b_gsb = [Buf(), Buf()]
                actT = sb(st, [128, 22, 512], BF16); b_actT = Buf()
                ot = mt; b_ot = b_mt
                bd = Buf()
                nw = 0
                for tb in range(8):
                    H = hh[tb % 2]; bH = b_hh[tb % 2]
                    S.dma("sp", H[:], hsrc[tb * 512:(tb + 1) * 512, :].rearrange("(i p) d -> p i d", p=128), writes=[bH])
                    for i in range(4):
                        ti = tb * 4 + i
                        k = i % 2
                        S.dma("sp", mt[k][:], mix[ti * 128:(ti + 1) * 128, :], writes=[b_mt[k]])
                        cp("pool", mb[k][:], mt[k][:], [b_mt[k]], [b_mb[k]])
                        pb, bpb = next_pb()
                        for c in range(8):
                            tr(pb[:, c, :], mb[k][:, c * 128:(c + 1) * 128], ident_b[:], [b_mb[k], b_c], [bpb])
                        cp(ev_eng(), mT[:, :, i * 128:(i + 1) * 128], pb[:], [bpb], [b_mT])
                    for i in range(4):
                        k = i % 2
                        for j in range(2):
                            ps, bps = next_ps()
                            for c in range(8):
                                mm(ps[:], mT[:, c, i * 128:(i + 1) * 128], wob[:, c, j * 512:(j + 1) * 512], c == 0, c == 7, [b_mT, bw], [bps])
                            tt("dve", H[:, i, j * 512:(j + 1) * 512], H[:, i, j * 512:(j + 1) * 512], ps[:], ALU.add, [bH, bps], [bH])
                        act(junk[:], H[:, i, :], AF.Square, [bH], [b_junk, b_ss[k]], accum_out=ss[k][:, 0:1])
                        rstd_of(ss[k][:], ss[k][:], 1.0 / D, [b_ss[k]], [b_ss[k]])
                        scl("pool", mb[k][:], H[:, i, :], ss[k][:, 0:1], [bH, b_ss[k]], [b_mb[k]])
                        pb, bpb = next_pb()
                        for c in range(8):
                            tr(pb[:, c, :], mb[k][:, c * 128:(c + 1) * 128], ident_b[:], [b_mb[k], b_c], [bpb])
                        cp(ev_eng(), hT[:, :, i * 128:(i + 1) * 128], pb[:], [bpb], [b_hT])
                    for grp in range(11):
                        kw = nw % 3; nw += 1
                        S.dma("sp", wg[kw][:, :, 0:256], wgub[:, grp * 256:(grp + 1) * 256].rearrange("(c p) n -> p c n", p=128), writes=[b_wg[kw]])
                        S.dma("sp", wg[kw][:, :, 256:512], wgub[:, DFF + grp * 256:DFF + (grp + 1) * 256].rearrange("(c p) n -> p c n", p=128),
                              writes=[b_wg[kw]])
                        for u in range(2):
                            f = grp * 2 + u
                            psg, bpsg = next_ps()
                            for c in range(8):
                                mm(psg[:], wg[kw][:, c, u * 128:(u + 1) * 128], hT[:, c, :], c == 0, c == 7, [b_wg[kw], b_hT], [bpsg])
                            psu, bpsu = next_ps()
                            for c in range(8):
                                mm(psu[:], wg[kw][:, c, 256 + u * 128:256 + (u + 1) * 128], hT[:, c, :], c == 0, c == 7, [b_wg[kw], b_hT], [bpsu])
                            act(gsb[u][:], psg[:], AF.Silu, [bpsg], [b_gsb[u]])
                            tt("dve", actT[:, f, :], gsb[u][:], psu[:], ALU.mult, [b_gsb[u], bpsu], [b_actT])
                    for i in range(4):
                        ti = tb * 4 + i
                        k = i % 2
                        for j in range(2):
                            ps, bps = next_ps()
                            for f in range(22):
                                mm(ps[:], actT[:, f, i * 128:(i + 1) * 128], wdb[:, f, j * 512:(j + 1) * 512], f == 0, f == 21, [b_actT, bw], [bps])
                            tt("dve", H[:, i, j * 512:(j + 1) * 512], H[:, i, j * 512:(j + 1) * 512], ps[:], ALU.add, [bH, bps], [bH])
                        if last:
                            act(junk[:], H[:, i, :], AF.Square, [bH], [b_junk, b_ss[k]], accum_out=ss[k][:, 0:1])
                            rstd_of(ss[k][:], ss[k][:], 1.0 / D, [b_ss[k]], [b_ss[k]])
                            stt(ot[k][:], H[:, i, :], ss[k][:, 0:1], fg[:], ALU.mult, ALU.mult, [bH, b_ss[k], bw], [b_ot[k]])
                            S.dma("pool", out[ti * 128:(ti + 1) * 128, :], ot[k][:], reads=[b_ot[k]], writes=[bd])
                    if not last:
                        S.dma("pool", hbuf[tb * 512:(tb + 1) * 512, :].rearrange("(i p) d -> p i d", p=128), H[:], reads=[bH], writes=[bd])


        def stage_rw(l):
            with ExitStack() as st:
                bw = Buf()

                def mk(shape, dt=F32):
                    return sb(st, shape, dt), Buf()

                def col2(src):
                    t = sb(st, [128, 2], F32)
                    S.dma("sp", t[:], src.rearrange("(c p) -> p c", p=128), writes=[bw], allow_slow_non_contiguous=True)
                    return t
                mu = sb(st, [128, 7], F32)
                S.dma("sp", mu[:], rw_mu[l].rearrange("(c p) -> p c", p=128), writes=[bw], allow_slow_non_contiguous=True)
                w0 = col2(rw_w0[l]); a0 = col2(rw_a0[l]); kkc = col2(rw_k_k[l]); kac = col2(rw_k_a[l]); rkc = col2(rw_r_k[l])
                nw0 = sb(st, [128, 2], F32)
                scl("dve", nw0[:], w0[:], -1.0, [bw], [bw])
                if l > 0:
                    v0 = col2(rw_v0[0])
                    v1t = sb(st, [128, 2, 32], F32)
                    S.dma("sp", v1t[:], rw_v1[0].rearrange("(c p) r -> p c r", p=128), writes=[bw])
                    v2t = sb(st, [32, 256], F32)
                    S.dma("sp", v2t[:], rw_v2[0], writes=[bw])
                lowW = sb(st, [64, 256], F32)
                S.dma("sp", lowW[0:32, :], rw_w2[l], writes=[bw])
                S.dma("sp", lowW[32:64, :], rw_a2[l], writes=[bw])
                g2s = sb(st, [128, 2, 64], F32)
                gw = sb(st, [128, 2, 64], F32)
                gb = sb(st, [128, 2, 64], F32)
                for hp in range(2):
                    for hh in range(2):
                        h = hp * 2 + hh
                        S.dma("sp", g2s[hh * 64:(hh + 1) * 64, hp, :], rw_g2[l, :, h * 64:(h + 1) * 64], writes=[bw])
                        S.dma("sp", gw[hh * 64:(hh + 1) * 64, hp, :], rw_gn_w[l, h * 64:(h + 1) * 64].partition_broadcast(64), writes=[bw])
                        S.dma("sp", gb[hh * 64:(hh + 1) * 64, hp, :], rw_gn_b[l, h * 64:(h + 1) * 64].partition_broadcast(64), writes=[bw])
                cm = sb(st, [128, 512], F32)
                S.op("pool", lambda e: e.memset(cm[:], 1.0), writes=[bw])
                S.op("pool", lambda e: e.memset(cm[:].rearrange("p (c t) -> p c t", t=64)[:, :, 0:1], 0.0), reads=[bw], writes=[bw])
                BDn = ("kap", "r", "b", "k", "rk", "v")
                BD = {}
                bBD = {}
                for hp in range(2):
                    for q in BDn:
                        BD[q, hp], bBD[q, hp] = mk([128, 8, 128])
                        S.op("pool", lambda e, t=BD[q, hp]: e.memset(t[:], 0.0), writes=[bBD[q, hp]])
                BDsg, bBDsg = mk([128, 8, 128])
                S.op("pool", lambda e: e.memset(BDsg[:], 0.0), writes=[bBDsg])
                Tst = []
                for hp in range(2):
                    t, b = mk([128, 64])
                    S.op("pool", lambda e, t=t: e.memset(t[:], 0.0), writes=[b])
                    Tst.append((t, b))
                gamC = [mk([128, 8]) for _ in range(2)]
                yseg = [mk([128, 8, 64]) for _ in range(2)]
                raw = [mk([128, 513]) for _ in range(7)]
                sh = [mk([128, 512]) for _ in range(7)]
                dtmp = mk([128, 512])
                sg0 = mk([64, 512])
                tmp = {n: mk([128, 512]) for n in ("e1", "ew", "lw", "a", "kk", "sq", "kkn", "t", "kmod", "b", "lgm", "gam", "gin", "gpr", "rk")}
                vv = mk([32, 512]); sgv = mk([128, 512]); vf = mk([128, 512])
                W = {}
                for hp in range(2):
                    for par in range(4):
                        d = {}
                        for n in ("A0", "A1", "B0", "B1", "Pm0", "Pm1", "Pt0", "Pt1", "AkT", "RBT", "RKT", "BbT", "BkT"):
                            d[n] = mk([128, 128])
                        for n in ("Xs", "Us", "yn", "jk", "Ysb"):
                            d[n] = mk([128, 64])
                        d["VG"] = mk([128, 130])
                        d["Vs"] = (d["VG"][0][:, 0:64], d["VG"][1])
                        d["Gs"] = (d["VG"][0][:, 64:128], d["VG"][1])
                        d["rks"] = (d["VG"][0][:, 128:130], d["VG"][1])
                        d["st"] = mk([128, 8])
                        W[hp, par] = d

                def v3(ap):
                    return ap.rearrange("p (c t) -> p c t", t=64)

                def pre(hp, c, par):
                    d = W[hp, par]
                    kap, r_, b_, k_, rk_, v_ = [BD[q, hp][:, c, :] for q in BDn]
                    bk = [bBD[q, hp] for q in BDn]
                    psA, bA = next_ps()
                    mm(psA[:, 0:128], b_, kap, True, True, [bk[2], bk[0]], [bA])
                    mm(psA[:, 128:256], kap, b_, True, True, [bk[2], bk[0]], [bA])
                    mm(psA[:, 256:384], k_, kap, True, True, [bk[3], bk[0]], [bA])
                    mm(psA[:, 384:512], b_, r_, True, True, [bk[2], bk[1]], [bA])
                    psB, bB = next_ps()
                    mm(psB[:, 0:128], k_, r_, True, True, [bk[3], bk[1]], [bB])
                    tt("dve", d["A0"][0][:], psA[:, 0:128], m_su[:], ALU.mult, [bA, b_c], [d["A0"][1]])
                    tt("dve", d["B0"][0][:], psA[:, 128:256], m_sl[:], ALU.mult, [bA, b_c], [d["B0"][1]])
                    tt("dve", d["AkT"][0][:], psA[:, 256:384], m_su[:], ALU.mult, [bA, b_c], [d["AkT"][1]])
                    tt("dve", d["RBT"][0][:], psA[:, 384:512], m_iu[:], ALU.mult, [bA, b_c], [d["RBT"][1]])
                    tt("dve", d["RKT"][0][:], psB[:, 0:128], m_iu[:], ALU.mult, [bB, b_c], [d["RKT"][1]])
                    psC, bC = next_ps()
                    mm(psC[:, 0:64], v_, istack[:], True, True, [bk[5], b_c], [bC])
                    mm(psC[:, 128:192], BDsg[:, c, :], g2s[:, hp, :], True, True, [bBDsg, bw], [bC])
                    mm(psC[:, 256:258], rk_, ones_f[:, 0:2], True, True, [bk[4], b_c], [bC])
                    cp("dve", d["VG"][0][:, 0:64], psC[:, 0:64], [bC], [d["VG"][1]])
                    cp("dve", d["VG"][0][:, 64:128], psC[:, 128:192], [bC], [d["VG"][1]])
                    cp("dve", d["VG"][0][:, 128:130], psC[:, 256:258], [bC], [d["VG"][1]])
                    tt("pool", d["Pm0"][0][:], ident_f[:], d["A0"][0][:], ALU.subtract, [b_c, d["A0"][1]], [d["Pm0"][1]])
                    tt("pool", d["Pt0"][0][:], ident_f[:], d["B0"][0][:], ALU.subtract, [b_c, d["B0"][1]], [d["Pt0"][1]])
                    yield
                    import os
                    pstop = int(os.environ.get("PRE_STOP", "9"))
                    if pstop <= 1:
                        return
                    psT, bT = next_ps()
                    tr(psT[:, 0:128], b_, ident_f[:], [bk[2], b_c], [bT])
                    tr(psT[:, 128:256], k_, ident_f[:], [bk[3], b_c], [bT])
                    cp("dve", d["BbT"][0][:], psT[:, 0:128], [bT], [d["BbT"][1]])
                    cp("dve", d["BkT"][0][:], psT[:, 128:256], [bT], [d["BkT"][1]])
                    yield
                    if pstop <= 2:
                        return
                    for j in range(5):
                        A, bAj = d["A%d" % (j % 2)]
                        B, bBj = d["B%d" % (j % 2)]
                        An, bAn = d["A%d" % ((j + 1) % 2)]
                        Bn, bBn = d["B%d" % ((j + 1) % 2)]
                        Pm, bPm = d["Pm%d" % (j % 2)]
                        Pt, bPt = d["Pt%d" % (j % 2)]
                        Pmn, bPmn = d["Pm%d" % ((j + 1) % 2)]
                        Ptn, bPtn = d["Pt%d" % ((j + 1) % 2)]
                        ps, bps = next_ps()
                        mm(ps[:, 0:128], B[:], A[:], True, True, [bBj, bAj], [bps])
                        if j < 4:
                            mm(ps[:, 128:256], A[:], B[:], True, True, [bBj, bAj], [bps])
                        cp("dve", An[:], ps[:, 0:128], [bps], [bAn])
                        if j < 4:
                            cp("dve", Bn[:], ps[:, 128:256], [bps], [bBn])
                        yield
                        ps2, bps2 = next_ps()
                        mm(ps2[:, 0:128], Pt[:], An[:], True, False, [bPt, bAn], [bps2])
                        mm(ps2[:, 0:128], Pt[:], ident_f[:], False, True, [bPt, b_c], [bps2])
                        if j < 4:
                            mm(ps2[:, 128:256], An[:], Pt[:], True, False, [bPt, bAn], [bps2])
                            mm(ps2[:, 128:256], ident_f[:], Pt[:], False, True, [bPt, b_c], [bps2])
                        cp("dve", Pmn[:], ps2[:, 0:128], [bps2], [bPmn])
                        if j < 4:
                            cp("dve", Ptn[:], ps2[:, 128:256], [bps2], [bPtn])
                        yield

                def seq(hp, c, par, seg):
                    d = W[hp, par]
                    kap, r_, b_, k_, rk_, v_ = [BD[q, hp][:, c, :] for q in BDn]
                    bk = [bBD[q, hp] for q in BDn]
                    T, bT_ = Tst[hp]
                    MT, bMT = d["Pm1"]
                    psX, bX = next_ps()
                    mm(psX[:, 0:64], kap, T[:], True, False, [bk[0], bT_], [bX])
                    mm(psX[:, 0:64], d["AkT"][0][:], d["Vs"][0], False, True, [d["AkT"][1], d["Vs"][1]], [bX])
                    cp("dve", d["Xs"][0][:], psX[:, 0:64], [bX], [d["Xs"][1]])
                    yield
                    psU, bU = next_ps()
                    mm(psU[:, 0:64], MT[:], d["Xs"][0][:], True, True, [bMT, d["Xs"][1]], [bU])
                    scl("dve", d["Us"][0][:], psU[:, 0:64], -1.0, [bU], [d["Us"][1]])
                    yield
                    psY, bY = next_ps()
                    mm(psY[:, 0:64], r_, T[:], True, False, [bk[1], bT_], [bY])
                    mm(psY[:, 0:64], d["RBT"][0][:], d["Us"][0][:], False, False, [d["RBT"][1], d["Us"][1]], [bY])
                    mm(psY[:, 0:64], d["RKT"][0][:], d["Vs"][0], False, True, [d["RKT"][1], d["Vs"][1]], [bY])
                    mm(psY[:, 128:192], d["BbT"][0][:], d["Us"][0][:], True, False, [d["BbT"][1], d["Us"][1]], [bY])
                    mm(psY[:, 128:192], d["BkT"][0][:], d["Vs"][0], False, False, [d["BkT"][1], d["Vs"][1]], [bY])
                    mm(psY[:, 128:192], ident_f[:], T[:], False, True, [b_c, bT_], [bY])
                    gC, bgC = gamC[hp]
                    scl("dve", T[:], psY[:, 128:192], gC[:, c:c + 1], [bY, bgC], [bT_])
                    stt_, bst = d["st"]
                    jk, bjk = d["jk"]
                    yn, byn = d["yn"]
                    Ysb, bYsb = d["Ysb"]
                    cp("dve", Ysb[:], psY[:, 0:64], [bY], [bYsb])
                    S.op("dve", lambda e: e.tensor_reduce(stt_[:, 0:1], Ysb[:], AX.X, ALU.add), reads=[bYsb], writes=[bst])
                    act(jk[:], Ysb[:], AF.Square, [bYsb], [bjk, bst], accum_out=stt_[:, 1:2])
                    scl("dve", stt_[:, 2:3], stt_[:, 0:1], 1.0 / 64, [bst], [bst])
                    tt("dve", stt_[:, 3:4], stt_[:, 2:3], stt_[:, 2:3], ALU.mult, [bst], [bst])
                    stt(stt_[:, 4:5], stt_[:, 1:2], 1.0 / 64, stt_[:, 3:4], ALU.mult, ALU.subtract, [bst], [bst])
                    rstd_of(stt_[:, 5:6], stt_[:, 4:5], 1.0, [bst], [bst], eps=64e-5)
                    S.op("dve", lambda e: e.tensor_scalar(yn[:], Ysb[:], stt_[:, 2:3], stt_[:, 5:6], ALU.subtract, ALU.mult),
                         reads=[bYsb, bst], writes=[byn])
                    tt("pool", yn[:], yn[:], gw[:, hp, :], ALU.mult, [byn, bw], [byn])
                    tt("pool", yn[:], yn[:], gb[:, hp, :], ALU.add, [byn, bw], [byn])
                    stt(yn[:], d["Vs"][0], d["rks"][0][:, 0:1], yn[:], ALU.mult, ALU.add, [d["Vs"][1], d["rks"][1], byn], [byn])
                    ys, bys = yseg[hp]
                    tt("pool", ys[:, c, :], yn[:], d["Gs"][0], ALU.mult, [byn, d["Gs"][1]], [bys])
                    if c == 7:
                        for hh in range(2):
                            h = hp * 2 + hh
                            S.dma("pool", mix[seg * 512:(seg + 1) * 512, h * 64:(h + 1) * 64].rearrange("(c t) v -> t c v", t=64),
                                  ys[hh * 64:(hh + 1) * 64, :, :], reads=[bys], writes=[bw])
                    yield

                def run(gens):
                    gens = list(gens)
                    while gens:
                        for g in list(gens):
                            try:
                                next(g)
                            except StopIteration:
                                gens.remove(g)

                import os
                for seg in range(int(os.environ.get('RW_SEGS', '8'))):
                    cs0 = seg * 512
                    cs = slice(cs0, cs0 + 512)
                    for m in range(7):
                        R_, bR = raw[m]
                        if seg == 0:
                            S.op("pool", lambda e, R_=R_: e.memset(R_[:, 0:1], 0.0), writes=[bR])
                            S.dma("sp", R_[:, 1:513], projT[m * 128:(m + 1) * 128, 0:512], writes=[bR])
                        else:
                            S.dma("sp", R_[:, :], projT[m * 128:(m + 1) * 128, cs0 - 1:cs0 + 512], writes=[bR])
                        tt(("dve", "pool")[m % 2], dtmp[0][:], R_[:, 0:512], R_[:, 1:513], ALU.subtract, [bR], [dtmp[1]])
                        stt(sh[m][0][:], dtmp[0][:], mu[:, m:m + 1], R_[:, 1:513], ALU.mult, ALU.add, [dtmp[1], bR, bw], [sh[m][1]])
                    lr, blr = sh[6]
                    act(lr[0:32, :], lr[0:32, :], AF.Tanh, [blr], [blr])
                    act(lr[64:128, :], lr[64:128, :], AF.Sigmoid, [blr], [blr])
                    S.dma("sp", sg0[0][:], lr[64:128, :], reads=[blr], writes=[sg0[1]])
                    cp("pool", BDsg[0:64, :, 0:64], v3(sg0[0][:]), [sg0[1]], [bBDsg])
                    cp("pool", BDsg[64:128, :, 64:128], v3(lr[64:128, :]), [blr], [bBDsg])
                    if l > 0:
                        ps, bps = next_ps()
                        for c in range(2):
                            mm(ps[0:32, :], v1t[:, c, :], sh[4 + c][0][:], c == 0, c == 1, [bw, sh[4 + c][1]], [bps])
                        cp("dve", vv[0][:], ps[0:32, :], [bps], [vv[1]])
                    for hp in range(2):
                        r, br = sh[hp]
                        k, bk_ = sh[2 + hp]
                        v, bv = sh[4 + hp]
                        if l > 0:
                            ps, bps = next_ps()
                            mm(ps[:], v2t[:, hp * 128:(hp + 1) * 128], vv[0][:], True, True, [bw, vv[1]], [bps])
                            act(sgv[0][:], ps[:], AF.Sigmoid, [bps, bw], [sgv[1]], bias=v0[:, hp:hp + 1])
                            S.dma("sp", vf[0][:], vfirst[hp * 128:(hp + 1) * 128, cs], writes=[vf[1]])
                            tt("dve", vf[0][:], vf[0][:], v[:], ALU.subtract, [vf[1], bv], [vf[1]])
                            tt("dve", vf[0][:], vf[0][:], sgv[0][:], ALU.mult, [vf[1], sgv[1]], [vf[1]])
                            tt("dve", v[:], v[:], vf[0][:], ALU.add, [vf[1], bv], [bv])
                        else:
                            S.dma("pool", vfirst[hp * 128:(hp + 1) * 128, cs], v[:], reads=[bv], writes=[bw])
                        e1, be1 = tmp["e1"]; ew, bew = tmp["ew"]; lw, blw = tmp["lw"]; a_, ba = tmp["a"]
                        kk, bkk = tmp["kk"]; sq, bsq = tmp["sq"]; kkn, bkkn = tmp["kkn"]; t_, bt_ = tmp["t"]
                        kmod, bkm = tmp["kmod"]; bb, bbb = tmp["b"]; lgm, blg = tmp["lgm"]; gam, bgam = tmp["gam"]
                        gin, bgin = tmp["gin"]; gpr, bgpr = tmp["gpr"]; rk, brk = tmp["rk"]
                        ps, bps = next_ps()
                        mm(ps[:], lowW[0:32, hp * 128:(hp + 1) * 128], lr[0:32, :], True, True, [bw, blr], [bps])
                        act(e1[:], ps[:], AF.Exp, [bps, bw], [be1], scale=-1.0, bias=nw0[:, hp:hp + 1])
                        act(e1[:], e1[:], AF.Ln, [be1], [be1], bias=1.0)
                        act(ew[:], e1[:], AF.Exp, [be1], [bew], scale=-1.0, bias=-0.5)
                        scl("pool", lw[:], ew[:], -1.0, [bew], [blw])
                        ps, bps = next_ps()
                        mm(ps[:], lowW[32:64, hp * 128:(hp + 1) * 128], lr[32:64, :], True, True, [bw, blr], [bps])
                        act(a_[:], ps[:], AF.Sigmoid, [bps, bw], [ba], bias=a0[:, hp:hp + 1])
                        scl("pool", kk[:], k[:], kkc[:, hp:hp + 1], [bk_, bw], [bkk])
                        tt("pool", sq[:], kk[:], kk[:], ALU.mult, [bkk], [bsq])
                        ps, bps = next_ps()
                        mm(ps[:], ones_bd[:], sq[:], True, True, [b_c, bsq], [bps])
                        S.op("dve", lambda e, ps=ps, sq=sq: e.tensor_scalar(sq[:], ps[:], 1e-24, None, ALU.max), reads=[bps], writes=[bsq])
                        act(sq[:], sq[:], AF.Ln, [bsq], [bsq])
                        act(sq[:], sq[:], AF.Exp, [bsq], [bsq], scale=-0.5)
                        tt("dve", kkn[:], kk[:], sq[:], ALU.mult, [bkk, bsq], [bkkn])
                        S.op("dve", lambda e, t_=t_, a_=a_, hp=hp: e.tensor_scalar(t_[:], a_[:], -1.0, kac[:, hp:hp + 1], ALU.add, ALU.mult),
                             reads=[ba, bw], writes=[bt_])
                        stt(kmod[:], t_[:], 1.0, k[:], ALU.add, ALU.mult, [bt_, bk_], [bkm])
                        tt("pool", bb[:], kkn[:], a_[:], ALU.mult, [bkkn, ba], [bbb])
                        S.op("dve", lambda e, lgm=lgm, lw=lw: e.tensor_tensor_scan(lgm[:], cm[:], lw[:], 0.0, ALU.mult, ALU.add),
                             reads=[bw, blw], writes=[blg])
                        act(gam[:], lgm[:], AF.Exp, [blg], [bgam])
                        act(gin[:], lgm[:], AF.Exp, [blg], [bgin], scale=-1.0)
                        tt("pool", gpr[:], lgm[:], lw[:], ALU.subtract, [blg, blw], [bgpr])
                        act(gpr[:], gpr[:], AF.Exp, [bgpr], [bgpr])
                        tt("pool", rk[:], r[:], kmod[:], ALU.mult, [br, bkm], [brk])
                        for hh in range(2):
                            rows = slice(hh * 64, hh * 64 + 64)
                            cb = slice(hh * 64, hh * 64 + 64)
                            tt("dve", BD["kap", hp][rows, :, cb], v3(kkn[rows, :]), v3(gpr[rows, :]), ALU.mult, [bkkn, bgpr], [bBD["kap", hp]])
                            tt("pool", BD["r", hp][rows, :, cb], v3(r[rows, :]), v3(gam[rows, :]), ALU.mult, [br, bgam], [bBD["r", hp]])
                            tt("dve", BD["b", hp][rows, :, cb], v3(bb[rows, :]), v3(gin[rows, :]), ALU.mult, [bbb, bgin], [bBD["b", hp]])
                            tt("pool", BD["k", hp][rows, :, cb], v3(kmod[rows, :]), v3(gin[rows, :]), ALU.mult, [bkm, bgin], [bBD["k", hp]])
                            scl("dve", BD["rk", hp][rows, :, cb], v3(rk[rows, :]), rkc[rows, hp:hp + 1], [brk, bw], [bBD["rk", hp]])
                            cp("act", BD["v", hp][rows, :, cb], v3(v[rows, :]), [bv], [bBD["v", hp]])
                        cp("pool", gamC[hp][0][:], v3(gam[:])[:, :, 63], [bgam], [gamC[hp][1]])
                    import os
                    rwm = os.environ.get("RW_MODE", "full")
                    if rwm == "prep":
                        continue
                    def seqchain(hp, cs_):
                        for c_ in cs_:
                            yield from seq(hp, c_, c_ % 4, seg)

                    if rwm != "full":
                        for c in range(8):
                            run([pre(0, c, c % 4), pre(1, c, c % 4)])
                        continue
                    for pc in range(4):
                        c0 = 2 * pc
                        gl = [pre(0, c0, c0 % 4), pre(1, c0, c0 % 4), pre(0, c0 + 1, (c0 + 1) % 4), pre(1, c0 + 1, (c0 + 1) % 4)]
                        if pc > 0:
                            gl += [seqchain(0, (c0 - 2, c0 - 1)), seqchain(1, (c0 - 2, c0 - 1))]
                        run(gl)
                    run([seqchain(0, (6, 7)), seqchain(1, (6, 7))])

        if stages is None:
            stages = ["rope"]
            for l_ in range(L):
                stages += ["p1w_%d" % l_, "ret_%d" % l_, "mlap_%d" % l_, "mlaa_%d" % l_, "rw_%d" % l_, "f_%d" % l_]
        for s_ in stages:
            if s_ == "rope":
                stage_rope()
            elif s_.startswith("p1w_"):
                stage_p1(int(s_[4:]), fuse_wgu=True)
            elif s_.startswith("p1_"):
                stage_p1(int(s_[3:]))
            elif s_.startswith("ret_"):
                stage_ret(int(s_[4:]))
            elif s_.startswith("mlap_"):
                stage_mlap(int(s_[5:]))
            elif s_.startswith("mlaa_"):
                stage_mlaa(int(s_[5:]))
            elif s_.startswith("rw_"):
                stage_rw(int(s_[3:]))
            elif s_.startswith("wgu_"):
                stage_wgu(int(s_[4:]))
            elif s_.startswith("f_"):
                stage_f(int(s_[2:]), int(s_[2:]) == L - 1)
            S.barrier()
        final = [b for bl in (b_out, b_mix, b_hbuf, b_projTM, b_rope) for b in bl] + [b for bl in b_projT for b in bl]
        S.finish(final)
    return nc


def _prep_inputs(inp):
    w_in = np.asarray(inp["w_in"])
    cols = list(range(896))
    rb = 896
    q = [rb + i for i in range(256)]
    kk = [rb + 256 + i for i in range(256)]

    def swap64(c):
        o = []
        for h in range(4):
            o += c[h * 64 + 32:h * 64 + 64] + c[h * 64:h * 64 + 32]
        return o
    cols += q + swap64(q) + kk + swap64(kk)
    mb = 1920
    cols += [mb + i for i in range(640)]
    kr = [mb + 640 + i for i in range(32)]
    cols += kr + kr[16:] + kr[:16]
    assert len(cols) == NFM
    cols += [rb + 512 + i for i in range(512)]
    w_in_p = np.ascontiguousarray(w_in[:, :, cols])
    wq = np.asarray(inp["mla_w_q_up"])
    qc = []
    for h in range(8):
        base = h * 96
        nope = [base + i for i in range(64)]
        rope = [base + 64 + i for i in range(32)]
        qc += nope + rope + nope + rope[16:] + rope[:16]
    wq_p = np.ascontiguousarray(wq[:, :, qc])
    wkv = np.asarray(inp["mla_w_kv_up"])
    kc = [h * 128 + i for h in range(8) for i in range(64)]
    vc = [h * 128 + 64 + i for h in range(8) for i in range(64)]
    r = np.arange(128)
    cst = np.stack([
        (10000.0 ** (-((r % 32) * 2).astype(np.float32) / 64)).astype(np.float32),
        np.where((r % 64) < 32, -1.0, 1.0).astype(np.float32),
        (10000.0 ** (-((r % 16) * 2).astype(np.float32) / 32)).astype(np.float32),
        np.where((r % 32) < 16, -1.0, 1.0).astype(np.float32)], axis=1).astype(np.float32)
    f = lambda k: np.ascontiguousarray(np.asarray(inp[k], dtype=np.float32))
    shared = {
        "cst": cst, "attn_norm": f("attn_norm"), "w_in_p": w_in_p, "w_out": f("w_out"),
        "rw_mu": f("rw_mu"), "rw_w0": f("rw_w0"), "rw_w2": f("rw_w2"), "rw_a0": f("rw_a0"), "rw_a2": f("rw_a2"),
        "rw_g2": f("rw_g2"), "rw_k_k": f("rw_k_k"), "rw_k_a": f("rw_k_a"), "rw_r_k": f("rw_r_k").reshape(L, 256),
        "rw_gn_w": f("rw_gn_w"), "rw_gn_b": f("rw_gn_b"), "rw_v0": f("rw_v0"), "rw_v1": f("rw_v1"), "rw_v2": f("rw_v2"),
        "q_norm": f("mla_q_norm"), "kv_norm": f("mla_kv_norm"), "wq_p": wq_p,
        "wkvk": np.ascontiguousarray(wkv[:, :, kc]), "wkvv": np.ascontiguousarray(wkv[:, :, vc]),
        "ffn_norm": f("ffn_norm"), "w_gu": f("w_gate_up"), "w_down": f("w_down"), "final_norm": f("final_norm"),
    }
    xs = np.asarray(inp["x"], dtype=np.float32)
    ps = np.asarray(inp["positions"]).astype(np.int32)
    maps = []
    for c in range(8):
        b = c % 4
        m = dict(shared)
        m["x"] = np.ascontiguousarray(xs[b])
        m["pos"] = np.ascontiguousarray(ps[b])
        maps.append(m)
    return maps


def kernel(**inputs):
    maps = _prep_inputs(inputs)
    nc = build()
    res = run_bass_kernel_spmd(nc, maps, core_ids=list(range(8)))
    return np.stack([res.results[b]["out"] for b in range(4)], axis=0).astype(np.float32)
```
